# Optimizing a Trainium2 kernel written in Bass

```python
import math
import jax
import jax.numpy as jnp
from jax import lax
import numpy as np

D_MODEL = 1024
BATCH = 32
SEQ = 2048
DEPTH = 1

N_HEADS = 8
N_KV_GROUPS = 2
HEADS_PER_GROUP = N_HEADS // N_KV_GROUPS
HEAD_DIM = 64
ATTN_WIDTH = N_HEADS * HEAD_DIM
KV_WIDTH = N_KV_GROUPS * HEAD_DIM
N_KV_SLOTS = 6
ROPE_DIM = HEAD_DIM // 4
ROPE_THETA = 500000.0
CMP_BLOCK = 32
CMP_STRIDE = 16
CMP_HIDDEN = 4 * HEAD_DIM
SEL_BLOCK = 64
N_SEL_BLOCKS = 16
WINDOW = 512
WIN_QBLOCK = 128
SEL_QBLOCK = 16
CONV_WIDTH = 512
CONV_KERNEL = 3
D_FF = 2816
N_BRANCHES = 2
W_IN_SIZES = (ATTN_WIDTH, N_KV_SLOTS * KV_WIDTH, 3 * N_HEADS, 3 * CONV_WIDTH, N_BRANCHES * D_MODEL)
W_IN_COLS = ATTN_WIDTH + N_KV_SLOTS * KV_WIDTH + 3 * N_HEADS + 3 * CONV_WIDTH + N_BRANCHES * D_MODEL

NORM_EPS = 1e-6
MASK_VALUE = -1e30
FORCE_SCORE = 1e4

kernel_name = "hybrid_nsa_shortconv_macaron"


def rms_norm(x, g):
    x32 = x.astype(jnp.float32)
    y = x32 * lax.rsqrt(jnp.mean(x32 * x32, axis=-1, keepdims=True) + NORM_EPS)
    return y.astype(x.dtype) * g


def swiglu(x, w_gate, w_up, w_down):
    return (jax.nn.silu(x @ w_gate) * (x @ w_up)) @ w_down


def partial_rope(x, pos):
    half = ROPE_DIM // 2
    inv_freq = ROPE_THETA ** (-jnp.arange(0, ROPE_DIM, 2, dtype=jnp.float32) / ROPE_DIM)
    ang = pos.astype(jnp.float32)[:, None] * inv_freq[None, :]
    cos = jnp.cos(ang)[:, None, :]
    sin = jnp.sin(ang)[:, None, :]
    xr = x[..., :ROPE_DIM].astype(jnp.float32)
    x1, x2 = xr[..., :half], xr[..., half:]
    rot = jnp.concatenate([x1 * cos - x2 * sin, x2 * cos + x1 * sin], axis=-1)
    return jnp.concatenate([rot.astype(x.dtype), x[..., ROPE_DIM:]], axis=-1)


def compress(t, pe, w1, w2):
    b, s, g, d = t.shape
    n_chunks = s // CMP_STRIDE
    n_per = CMP_BLOCK // CMP_STRIDE
    nb = n_chunks - n_per + 1
    chunks = t.reshape(b, n_chunks, CMP_STRIDE, g, d)
    blocks = jnp.concatenate([chunks[:, j:j + nb] for j in range(n_per)], axis=2)
    blocks = blocks + pe[None, None, :, None, :]
    flat = blocks.transpose(0, 1, 3, 2, 4).reshape(b, nb, g, CMP_BLOCK * d)
    return jax.nn.silu(flat @ w1) @ w2


def compressed_attention(q, k_c, v_c, pos):
    nb = k_c.shape[2]
    cmp_end = jnp.arange(nb) * CMP_STRIDE + CMP_BLOCK - 1
    valid = cmp_end[None, :] <= pos[:, None]
    s = jnp.einsum('bghsd,bgnd->bghsn', q, k_c).astype(jnp.float32) * (HEAD_DIM ** -0.5)
    p = jax.nn.softmax(jnp.where(valid, s, MASK_VALUE), axis=-1) * valid.astype(jnp.float32)
    o = jnp.einsum('bghsn,bgnd->bghsd', p.astype(v_c.dtype), v_c)
    return o, p


def select_blocks(p_c, pos):
    s = pos.shape[0]
    nb = p_c.shape[-1]
    nsb = s // SEL_BLOCK
    k_eff = min(N_SEL_BLOCKS, nsb)
    cs = jnp.arange(nb) * CMP_STRIDE
    ce = cs + CMP_BLOCK
    ss = jnp.arange(nsb) * SEL_BLOCK
    se = ss + SEL_BLOCK
    overlap = ((cs[:, None] < se[None, :]) & (ce[:, None] > ss[None, :])).astype(jnp.float32)
    imp = jnp.einsum('bghsn,nj->bgsj', p_c, overlap)
    j = jnp.arange(nsb)[None, :]
    cur = (pos // SEL_BLOCK)[:, None]
    forced = (j == 0) | (j == cur) | (j == cur - 1)
    future = ss[None, :] > pos[:, None]
    imp = jnp.where(forced, FORCE_SCORE, jnp.where(future, -1.0, imp))
    _, idx = lax.top_k(imp, k_eff)
    return idx


def selected_attention(q, k_s, v_s, sel_idx):
    b, g, hp, s, d = q.shape
    kk = sel_idx.shape[-1]
    nsb = s // SEL_BLOCK
    n_qc = s // SEL_QBLOCK
    k_blk = k_s.reshape(b, g, nsb, SEL_BLOCK, d)
    v_blk = v_s.reshape(b, g, nsb, SEL_BLOCK, d)
    q_ch = q.reshape(b, g, hp, n_qc, SEL_QBLOCK, d).transpose(3, 0, 1, 2, 4, 5)
    i_ch = sel_idx.reshape(b, g, n_qc, SEL_QBLOCK, kk).transpose(2, 0, 1, 3, 4)
    starts = jnp.arange(n_qc) * SEL_QBLOCK
    b_ix = jnp.arange(b)[:, None, None, None]
    g_ix = jnp.arange(g)[None, :, None, None]

    def step(args):
        qc, ic, st = args
        kg = k_blk[b_ix, g_ix, ic]
        vg = v_blk[b_ix, g_ix, ic]
        sc = jnp.einsum('bghqd,bgqkld->bghqkl', qc, kg).astype(jnp.float32) * (HEAD_DIM ** -0.5)
        kpos = ic[..., None] * SEL_BLOCK + jnp.arange(SEL_BLOCK)
        qpos = st + jnp.arange(SEL_QBLOCK)
        valid = kpos <= qpos[None, None, :, None, None]
        sc = jnp.where(valid[:, :, None], sc, MASK_VALUE).reshape(b, g, hp, SEL_QBLOCK, kk * SEL_BLOCK)
        p = jax.nn.softmax(sc, axis=-1).reshape(b, g, hp, SEL_QBLOCK, kk, SEL_BLOCK)
        return jnp.einsum('bghqkl,bgqkld->bghqd', p.astype(vg.dtype), vg)

    o = lax.map(step, (q_ch, i_ch, starts))
    return o.transpose(1, 2, 3, 0, 4, 5).reshape(b, g, hp, s, d)


def window_attention(q, k_w, v_w):
    b, g, hp, s, d = q.shape
    n_wb = s // WIN_QBLOCK
    span = WINDOW + WIN_QBLOCK
    k_pad = jnp.pad(k_w, ((0, 0), (0, 0), (WINDOW, 0), (0, 0)))
    v_pad = jnp.pad(v_w, ((0, 0), (0, 0), (WINDOW, 0), (0, 0)))
    q_bl = q.reshape(b, g, hp, n_wb, WIN_QBLOCK, d).transpose(3, 0, 1, 2, 4, 5)
    starts = jnp.arange(n_wb) * WIN_QBLOCK

    def step(args):
        qb, st = args
        kb = lax.dynamic_slice_in_dim(k_pad, st, span, axis=2)
        vb = lax.dynamic_slice_in_dim(v_pad, st, span, axis=2)
        sc = jnp.einsum('bghqd,bgkd->bghqk', qb, kb).astype(jnp.float32) * (HEAD_DIM ** -0.5)
        kpos = st - WINDOW + jnp.arange(span)
        qpos = st + jnp.arange(WIN_QBLOCK)
        diff = qpos[:, None] - kpos[None, :]
        valid = (kpos[None, :] >= 0) & (diff >= 0) & (diff < WINDOW)
        p = jax.nn.softmax(jnp.where(valid, sc, MASK_VALUE), axis=-1)
        return jnp.einsum('bghqk,bgkd->bghqd', p.astype(vb.dtype), vb)

    o = lax.map(step, (q_bl, starts))
    return o.transpose(1, 2, 3, 0, 4, 5).reshape(b, g, hp, s, d)


def nsa_mixer(q, kv, gate_logits, q_norm_g, k_norm_g, cmp_pe_k, cmp_pe_v,
              cmp_k_w1, cmp_k_w2, cmp_v_w1, cmp_v_w2):
    b, s, _ = q.shape
    pos = jnp.arange(s)
    q = partial_rope(rms_norm(q.reshape(b, s, N_HEADS, HEAD_DIM), q_norm_g), pos)
    q = q.reshape(b, s, N_KV_GROUPS, HEADS_PER_GROUP, HEAD_DIM).transpose(0, 2, 3, 1, 4)
    kv = kv.reshape(b, s, N_KV_SLOTS, N_KV_GROUPS, HEAD_DIM)
    k_c = rms_norm(compress(kv[:, :, 0], cmp_pe_k, cmp_k_w1, cmp_k_w2), k_norm_g[0]).transpose(0, 2, 1, 3)
    v_c = compress(kv[:, :, 1], cmp_pe_v, cmp_v_w1, cmp_v_w2).transpose(0, 2, 1, 3)
    k_s = partial_rope(rms_norm(kv[:, :, 2], k_norm_g[1]), pos).transpose(0, 2, 1, 3)
    v_s = kv[:, :, 3].transpose(0, 2, 1, 3)
    k_w = partial_rope(rms_norm(kv[:, :, 4], k_norm_g[2]), pos).transpose(0, 2, 1, 3)
    v_w = kv[:, :, 5].transpose(0, 2, 1, 3)

    o_c, p_c = compressed_attention(q, k_c, v_c, pos)
    sel_idx = select_blocks(p_c, pos)
    o_s = selected_attention(q, k_s, v_s, sel_idx)
    o_w = window_attention(q, k_w, v_w)

    gts = jax.nn.sigmoid(gate_logits.reshape(b, s, 3, N_KV_GROUPS, HEADS_PER_GROUP).astype(jnp.float32))
    gts = gts.astype(q.dtype).transpose(2, 0, 3, 4, 1)[..., None]
    o = gts[0] * o_c + gts[1] * o_s + gts[2] * o_w
    return o.transpose(0, 3, 1, 2, 4).reshape(b, s, ATTN_WIDTH)


def short_conv_mixer(cv):
    return jnp.split(cv, 3, axis=-1)


def setup_inputs(seed: int = 0) -> dict:
    key = jax.random.key(seed)
    ks = jax.random.split(key, 24)
    L = DEPTH

    def nrm(k, shape, fan_in):
        return jax.random.normal(k, shape, jnp.float32) * (fan_in ** -0.5)

    def gain(k, shape):
        return 1.0 + 0.02 * jax.random.normal(k, shape, jnp.float32)

    return {
        "x": jax.random.normal(ks[0], (BATCH, SEQ, D_MODEL), jnp.float32),
        "ffn1_norm_g": gain(ks[1], (L, D_MODEL)),
        "ffn1_w_gate": nrm(ks[2], (L, D_MODEL, D_FF), D_MODEL),
        "ffn1_w_up": nrm(ks[3], (L, D_MODEL, D_FF), D_MODEL),
        "ffn1_w_down": nrm(ks[4], (L, D_FF, D_MODEL), D_FF),
        "mix_norm_g": gain(ks[5], (L, D_MODEL)),
        "w_in": nrm(ks[6], (L, D_MODEL, W_IN_COLS), D_MODEL),
        "q_norm_g": gain(ks[7], (L, HEAD_DIM)),
        "k_norm_g": gain(ks[8], (L, 3, HEAD_DIM)),
        "cmp_pe_k": 0.1 * jax.random.normal(ks[9], (L, CMP_BLOCK, HEAD_DIM), jnp.float32),
        "cmp_pe_v": 0.1 * jax.random.normal(ks[10], (L, CMP_BLOCK, HEAD_DIM), jnp.float32),
        "cmp_k_w1": nrm(ks[11], (L, CMP_BLOCK * HEAD_DIM, CMP_HIDDEN), CMP_BLOCK * HEAD_DIM),
        "cmp_k_w2": nrm(ks[12], (L, CMP_HIDDEN, HEAD_DIM), CMP_HIDDEN),
        "cmp_v_w1": nrm(ks[13], (L, CMP_BLOCK * HEAD_DIM, CMP_HIDDEN), CMP_BLOCK * HEAD_DIM),
        "cmp_v_w2": nrm(ks[14], (L, CMP_HIDDEN, HEAD_DIM), CMP_HIDDEN),
        "conv_w": nrm(ks[15], (L, CONV_KERNEL, CONV_WIDTH), CONV_KERNEL),
        "w_attn_branch": nrm(ks[16], (L, ATTN_WIDTH, D_MODEL), ATTN_WIDTH),
        "w_conv_branch": nrm(ks[17], (L, CONV_WIDTH, D_MODEL), CONV_WIDTH),
        "w_out": nrm(ks[18], (L, D_MODEL, D_MODEL), D_MODEL),
        "ffn2_norm_g": gain(ks[19], (L, D_MODEL)),
        "ffn2_w_gate": nrm(ks[20], (L, D_MODEL, D_FF), D_MODEL),
        "ffn2_w_up": nrm(ks[21], (L, D_MODEL, D_FF), D_MODEL),
        "ffn2_w_down": nrm(ks[22], (L, D_FF, D_MODEL), D_FF),
    }


def reference(x, ffn1_norm_g, ffn1_w_gate, ffn1_w_up, ffn1_w_down, mix_norm_g, w_in,
              q_norm_g, k_norm_g, cmp_pe_k, cmp_pe_v, cmp_k_w1, cmp_k_w2, cmp_v_w1, cmp_v_w2,
              conv_w, w_attn_branch, w_conv_branch, w_out,
              ffn2_norm_g, ffn2_w_gate, ffn2_w_up, ffn2_w_down):
    split_at = [int(v) for v in np.cumsum(W_IN_SIZES)[:-1]]
    for l in range(DEPTH):
        x = x + 0.5 * swiglu(rms_norm(x, ffn1_norm_g[l]), ffn1_w_gate[l], ffn1_w_up[l], ffn1_w_down[l])

        h = rms_norm(x, mix_norm_g[l])
        proj = h @ w_in[l]
        q, kv, nsa_gl, cv, merge_gl = jnp.split(proj, split_at, axis=-1)

        a = nsa_mixer(q, kv, nsa_gl, q_norm_g[l], k_norm_g[l], cmp_pe_k[l], cmp_pe_v[l],
                      cmp_k_w1[l], cmp_k_w2[l], cmp_v_w1[l], cmp_v_w2[l])

        gate_b, gate_c, u = short_conv_mixer(cv)
        conv = lax.conv_general_dilated(
            gate_c * u, conv_w[l][:, None, :], window_strides=(1,),
            padding=[(CONV_KERNEL - 1, 0)], dimension_numbers=('NWC', 'WIO', 'NWC'),
            feature_group_count=CONV_WIDTH)
        c = gate_b * conv

        mg = jax.nn.sigmoid(merge_gl.astype(jnp.float32)).astype(x.dtype)
        g_a, g_c = mg[..., :D_MODEL], mg[..., D_MODEL:]
        merged = g_a * (a @ w_attn_branch[l]) + g_c * (c @ w_conv_branch[l])
        x = x + merged @ w_out[l]

        x = x + 0.5 * swiglu(rms_norm(x, ffn2_norm_g[l]), ffn2_w_gate[l], ffn2_w_up[l], ffn2_w_down[l])
    return x
```

```python
import numpy as np
from contextlib import ExitStack
import concourse.bass as bass
import concourse.mybir as mybir
from concourse.bass_utils import run_bass_kernel_spmd

F32 = mybir.dt.float32
BF = mybir.dt.bfloat16
AF = mybir.ActivationFunctionType
ALU = mybir.AluOpType
AX = mybir.AxisListType

S = 2048
D = 1024
DFF = 2816
NCH = 16
NEG = -30000.0
EPS = 1e-6
N_CORES = 8

ENG_ATTR = {"pe": "tensor", "act": "scalar", "dve": "vector", "pool": "gpsimd", "sp": "sync"}
N_DMA_SEMS = 16
EPOCH = 20000


class Buf:
    __slots__ = ("name", "lw", "rd")

    def __init__(self, name):
        self.name = name
        self.lw = None
        self.rd = []


class Op:
    __slots__ = ("eng", "fn", "deps", "dma", "sig", "sem", "val", "idx", "qpos")

    def __init__(self, eng, fn, dma):
        self.eng = eng
        self.fn = fn
        self.deps = []
        self.dma = dma
        self.sig = False
        self.sem = None
        self.val = 0
        self.idx = 0
        self.qpos = 0


class Prog:
    def __init__(self, nc, es):
        self.nc = nc
        self.es = es
        self.ops = []
        self.nbuf = 0
        self.allbufs = []
        self.last_fence = None

    def buf(self, name=None, arena=True):
        self.nbuf += 1
        b = Buf(name or f"b{self.nbuf}")
        if arena:
            self.allbufs.append(b)
            b.lw = self.last_fence
        return b

    def bufs(self, n, name="b", arena=True):
        return [self.buf(f"{name}{i}", arena) for i in range(n)]

    def add(self, eng, fn, r=(), w=(), dma=False):
        op = Op(eng, fn, dma)
        op.idx = len(self.ops)
        deps = set()
        for b in r:
            if b.lw is not None:
                deps.add(b.lw)
        for b in w:
            if b.lw is not None:
                deps.add(b.lw)
            for x in b.rd:
                deps.add(x)
        deps.discard(op.idx)
        op.deps = sorted(deps)
        for b in r:
            b.rd.append(op.idx)
        for b in w:
            b.lw = op.idx
            b.rd = []
        self.ops.append(op)
        return op

    def dma(self, q, fn, r=(), w=()):
        return self.add(q, fn, r, w, dma=True)

    def emit(self):
        nc = self.nc
        ops = self.ops
        cnt = {}
        for op in ops:
            op.qpos = cnt.get(op.eng, 0)
            cnt[op.eng] = op.qpos + 1
        for op in ops:
            keep = []
            for d in op.deps:
                p = ops[d]
                if not p.dma and p.eng == op.eng and not op.dma:
                    if p.eng == "pe":
                        continue
                    if p.eng != "pool" and op.qpos - p.qpos > 3:
                        continue
                p.sig = True
                keep.append(d)
            op.deps = keep
        sem_ctr = [0]

        def new_sem(tag):
            sem_ctr[0] += 1
            return self.es.enter_context(nc.semaphore(f"{tag}{sem_ctr[0]}"))

        eng_sem = {}
        eng_cnt = {}
        dq = {}
        all_dma_last = []
        for op in ops:
            if op.dma:
                if op.eng not in dq:
                    dq[op.eng] = {"sems": [new_sem("dq" + op.eng) for _ in range(N_DMA_SEMS)],
                                  "val": [0] * N_DMA_SEMS, "prev": [None] * N_DMA_SEMS, "n": 0}
                q = dq[op.eng]
                s = q["n"] % N_DMA_SEMS
                q["n"] += 1
                if q["prev"][s] is not None:
                    op.deps.append(q["prev"][s])
                q["val"][s] += 16
                op.sem = q["sems"][s]
                op.val = q["val"][s]
                op.sig = True
                q["prev"][s] = op.idx
            elif op.sig:
                e = op.eng
                if e not in eng_sem or eng_cnt[e] >= EPOCH:
                    eng_sem[e] = new_sem(e)
                    eng_cnt[e] = 0
                eng_cnt[e] += 1
                op.sem = eng_sem[e]
                op.val = eng_cnt[e]
        self.n_sems = sem_ctr[0]
        last_dma = [p for q in dq.values() for p in q["prev"] if p is not None]
        by_eng = {}
        for op in ops:
            by_eng.setdefault(op.eng, []).append(op)
        assert "sp" in by_eng
        block = self.es.enter_context(nc.Block())

        def make_section(lst, is_last_waiter):
            def section(e):
                known = {}
                for op in lst:
                    for d in op.deps:
                        p = ops[d]
                        k = id(p.sem)
                        if known.get(k, 0) >= p.val:
                            continue
                        e.wait_ge(p.sem, p.val)
                        known[k] = p.val
                    ins = op.fn(e)
                    if op.sig:
                        ins.then_inc(op.sem, 16 if op.dma else 1)
                if is_last_waiter:
                    for d in last_dma:
                        p = ops[d]
                        e.wait_ge(p.sem, p.val)
            return section

        for ename, lst in by_eng.items():
            getattr(block, ENG_ATTR[ename])(make_section(lst, ename == "sp"))


def _const_tables():
    c = {}
    pos = np.arange(S)
    inv_freq = (500000.0 ** (-np.arange(0, 16, 2, dtype=np.float32) / 16)).astype(np.float32)
    ang = pos.astype(np.float32)[:, None] * inv_freq[None, :]
    cs = np.cos(ang).astype(np.float32).reshape(NCH, 128, 8).transpose(1, 0, 2)
    sn = np.sin(ang).astype(np.float32).reshape(NCH, 128, 8).transpose(1, 0, 2)
    c["c_cos"] = np.ascontiguousarray(cs)
    c["c_sin"] = np.ascontiguousarray(sn)
    k = np.arange(128)[:, None]
    q = np.arange(128)[None, :]
    c["c_tric"] = np.where(k <= q, 0.0, NEG).astype(np.float32)
    c["c_triw"] = np.where(k > q, 0.0, NEG).astype(np.float32)
    c["c_tric01"] = (k <= q).astype(np.float32)
    c["c_triw01"] = (k > q).astype(np.float32)
    n = np.arange(127)[:, None, None]
    T = np.arange(4)[None, :, None]
    ql = np.arange(512)[None, None, :]
    c["c_cmask"] = (16 * n + 31 <= 512 * T + ql).astype(np.float32)
    qq = (np.arange(NCH)[None, :, None] * 128 + np.arange(128)[:, None, None])
    j = np.arange(32)[None, None, :]
    cur = qq // 64
    f0 = (j == 0)
    f1 = (j == cur)
    f2 = (j == cur - 1)
    forced = f0 | f1 | f2
    future = (64 * j > qq)
    c["c_m1"] = np.where(forced | future, 0.0, 1.0).astype(np.float32)
    a1 = np.where(future, -(1.0 + j), 0.0)
    a1 = np.where(f2, 3.0e4, a1)
    a1 = np.where(f1, 2.0e4, a1)
    a1 = np.where(f0, 1.0e4, a1)
    c["c_a1"] = a1.astype(np.float32)
    nn = np.arange(127)[:, None]
    jj = np.arange(32)[None, :]
    ovl = ((16 * nn < 64 * jj + 64) & (16 * nn + 32 > 64 * jj)).astype(np.float32)
    va = np.concatenate([np.ones((127, 1), np.float32), ovl], axis=1)
    c["c_vca"] = np.ascontiguousarray(np.broadcast_to(va[:, None, :], (127, 2, 33))).astype(np.float32)
    e = (np.arange(S)[None, :] // 64 == np.arange(32)[:, None]).astype(np.float32)
    c["c_efull"] = e
    c["c_ident"] = np.eye(128, dtype=np.float32)
    return c


def build(nseq=4, dbg=False, phases="ABCDEFHIJ"):
    nc = bass.Bass("TRN2", target_bir_lowering=False)
    es = ExitStack()
    P = Prog(nc, es)

    def din(name, shape):
        return nc.dram_tensor(name, list(shape), F32, kind="ExternalInput").ap()

    x_d = din("x", [nseq, S, D])
    out_d = nc.dram_tensor("out", [nseq, S, D], F32, kind="ExternalOutput").ap()
    wg_d = [din("w_gate1", [D, DFF]), din("w_gate2", [D, DFF])]
    wu_d = [din("w_up1", [D, DFF]), din("w_up2", [D, DFF])]
    wd_d = [din("w_down1", [DFF, D]), din("w_down2", [DFF, D])]
    win_d = din("w_in", [D, 4888])
    w1_d = [din("cmp_k_w1", [2048, 256]), din("cmp_v_w1", [2048, 256])]
    w2_d = [din("cmp_k_w2", [256, 64]), din("cmp_v_w2", [256, 64])]
    wa_d = din("w_a", [512, D])
    wc_d = din("w_c", [512, D])
    wo_d = din("w_o", [D, D])
    gT_d = din("gT", [128, 3, 8])
    gqk_d = din("gqk", [128, 768])
    gkc_d = din("gkc", [128, 128])
    peT_d = din("peT", [64, 2, 32])
    convw_d = din("convw", [128, 4, 3])
    cd = {k: din(k, v.shape) for k, v in _const_tables().items()}
    dbg_d = {}
    if dbg:
        for nm, shp in [("d_x1", [S, D]), ("d_qm", [96, 8, S]), ("d_ke", [96, 4, S]), ("d_v", [128, 16, 4, 65]),
                        ("d_kct", [64, 2, 127]), ("d_vca", [127, 2, 97]), ("d_gates", [128, 16, 24]),
                        ("d_aT", [128, 4, S]), ("d_cT", [128, 4, S]), ("d_x2", [S, D]), ("d_kvc", [128, 2, S])]:
            dbg_d[nm] = nc.dram_tensor(nm, shp, F32, kind="ExternalOutput").ap()

    def sb(name, shape, dt):
        return es.enter_context(nc.sbuf_tensor("sb_" + name, list(shape), dt))

    x_sb = sb("x_sb", [128, NCH, D], F32)
    xb = P.bufs(NCH, "x", arena=False)
    ident_bf = sb("ident_bf", [128, 128], BF)
    ident_f = sb("ident_f", [128, 128], F32)
    tric = sb("tric", [128, 128], BF)
    triw = sb("triw", [128, 128], BF)
    tric01 = sb("tric01", [128, 128], BF)
    triw01 = sb("triw01", [128, 128], BF)
    cmask = sb("cmask", [127, 4, 512], BF)
    m1_sb = sb("m1", [128, NCH, 32], F32)
    a1_sb = sb("a1", [128, NCH, 32], F32)
    cos_sb = sb("cos", [128, NCH, 8], F32)
    sin_sb = sb("sin", [128, NCH, 8], F32)
    gT_sb = sb("gT", [128, 3, 8], F32)
    gqk_sb = sb("gqk", [128, 768], F32)
    gkc_sb = sb("gkc", [128, 128], F32)
    convw_sb = sb("convw", [128, 4, 3], F32)
    peT_sb = sb("peT", [64, 2, 32], BF)
    w2_sb = sb("w2", [128, 2, 2, 64], BF)
    pbias_sb = sb("pbias", [128, 2, 2], F32)
    stat = sb("stat", [128, NCH, 4], F32)
    fence_t = sb("fence_t", [128, 8], F32)
    cb = P.buf("consts", arena=False)
    statb = P.bufs(NCH, "stat", arena=False)

    AR_F32 = 29696
    arena = sb("arena", [128, AR_F32], F32)

    def AV(off_kb, shape, dt, np_=128):
        o = int(round(off_kb * 256))
        n = 1
        for s_ in shape[1:]:
            n *= s_
        if dt == BF:
            assert n % 2 == 0
            n32 = n // 2
        else:
            n32 = n
        assert o + n32 <= AR_F32, (off_kb, shape)
        v = arena[0:np_, o:o + n32]
        if dt == BF:
            v = v.bitcast(BF)
        if len(shape) == 3:
            v = v.rearrange("p (a b) -> p a b", a=shape[1])
        elif len(shape) == 4:
            v = v.rearrange("p (a b c) -> p a b c", a=shape[1], b=shape[2])
        return v

    ps = es.enter_context(nc.psum_tensor("ps", [128, 4096], F32))
    pb = P.bufs(8, "psum", arena=False)
    bank_ctr = {"all": 0, "hi": 0, "pair": 0}

    def bank(pool="all"):
        if pool == "all":
            i = bank_ctr["all"] % 8
            bank_ctr["all"] += 1
        else:
            i = 4 + bank_ctr["hi"] % 4
            bank_ctr["hi"] += 1
        return i

    def pair():
        i = (bank_ctr["pair"] % 2) * 2
        bank_ctr["pair"] += 1
        return i

    def PS(i, n=512):
        return ps[:, i * 512:i * 512 + n]

    def PSB(i):
        return ps[:, i * 512:(i + 1) * 512].bitcast(BF)

    def mm(out, lhsT, rhs, start, stop, r, w):
        P.add("pe", lambda e: e.matmul(out, lhsT=lhsT, rhs=rhs, start=start, stop=stop, skip_group_check=True), r=r, w=w)

    def tr(out, in_, ident, r, w):
        P.add("pe", lambda e: e.transpose(out=out, in_=in_, identity=ident), r=r, w=w)

    def act(out, in_, func, r, w, scale=None, bias=None, accum=None):
        kw = {}
        if scale is not None:
            kw["scale"] = scale
        if bias is not None:
            kw["bias"] = bias
        if accum is not None:
            kw["accum_out"] = accum
        P.add("act", lambda e: e.activation(out=out, in_=in_, func=func, **kw), r=r, w=w)

    def tt(out, in0, in1, op, r, w, eng="dve"):
        P.add(eng, lambda e: e.tensor_tensor(out=out, in0=in0, in1=in1, op=op), r=r, w=w)

    def ts(out, in0, s1, op0, r, w, s2=None, op1=None, eng="dve"):
        if op1 is None:
            P.add(eng, lambda e: e.tensor_scalar(out=out, in0=in0, scalar1=s1, scalar2=None, op0=op0), r=r, w=w)
        else:
            P.add(eng, lambda e: e.tensor_scalar(out=out, in0=in0, scalar1=s1, scalar2=s2, op0=op0, op1=op1), r=r, w=w)

    def stt(out, in0, scalar, in1, op0, op1, r, w, eng="dve"):
        P.add(eng, lambda e: e.scalar_tensor_tensor(out=out, in0=in0, scalar=scalar, in1=in1, op0=op0, op1=op1), r=r, w=w)

    def cp(out, in_, r, w, eng="dve"):
        P.add(eng, lambda e: e.tensor_copy(out=out, in_=in_), r=r, w=w)

    def recip(out, in_, r, w):
        P.add("dve", lambda e: e.reciprocal(out=out, in_=in_), r=r, w=w)

    def red(out, in_, r, w):
        P.add("dve", lambda e: e.tensor_reduce(out=out, in_=in_, axis=AX.X, op=ALU.add), r=r, w=w)

    def memset(ap, val, r, w, eng="dve"):
        P.add(eng, lambda e: e.memset(ap, val), r=r, w=w)

    def dma(q, out, in_, r, w):
        P.dma(q, lambda e: e.dma_start(out=out, in_=in_), r=r, w=w)

    def fence():
        op = P.add("dve", lambda e: e.memset(fence_t[:, 0:1], 0.0), r=(), w=list(P.allbufs))
        P.last_fence = op.idx
        P.allbufs = [b for b in P.allbufs if b.name in ("QM", "KE", "VT", "GT", "KVC", "KCT", "VCA", "AT", "CT") or b.name.startswith("ffn_") or b.name.startswith("mixhT")]

    def bc(ap, shape):
        return ap.to_broadcast(list(shape))

    dma("pool", ident_bf[:], cd["c_ident"], [], [cb])
    dma("sp", ident_f[:], cd["c_ident"], [], [cb])
    dma("pool", tric[:], cd["c_tric"], [], [cb])
    dma("pool", triw[:], cd["c_triw"], [], [cb])
    dma("pool", tric01[:], cd["c_tric01"], [], [cb])
    dma("pool", triw01[:], cd["c_triw01"], [], [cb])
    dma("pool", cmask[:], cd["c_cmask"], [], [cb])
    dma("sp", m1_sb[:], cd["c_m1"], [], [cb])
    dma("sp", a1_sb[:], cd["c_a1"], [], [cb])
    dma("sp", cos_sb[:], cd["c_cos"], [], [cb])
    dma("sp", sin_sb[:], cd["c_sin"], [], [cb])
    dma("sp", gT_sb[:], gT_d, [], [cb])
    dma("sp", gqk_sb[:], gqk_d, [], [cb])
    dma("sp", gkc_sb[:], gkc_d, [], [cb])
    dma("sp", convw_sb[:], convw_d, [], [cb])
    dma("pool", peT_sb[:], peT_d, [], [cb])
    for slot in range(2):
        dma("pool", w2_sb[:, slot, :, :], w2_d[slot].rearrange("(hc p) d -> p hc d", p=128), [], [cb])

    def norm_chunk(c, gi, xhat_v, xhat_b, junk_v, junk_b, dst, dst_b):
        xs = x_sb[:, c, :]
        st = stat[:, c, :]
        act(junk_v, xs, AF.Square, [xb[c]], [junk_b, statb[c]], accum=st[:, 0:1])
        act(st[:, 1:2], st[:, 0:1], AF.Sqrt, [statb[c]], [statb[c]], scale=1.0 / D, bias=EPS)
        recip(st[:, 2:3], st[:, 1:2], [statb[c]], [statb[c]])
        act(xhat_v, xs, AF.Copy, [xb[c], statb[c]], [xhat_b], scale=st[:, 2:3])
        norm_b(gi, xhat_v, xhat_b, dst, dst_b)

    def norm_a(c, xhat_v, xhat_b, junk_v, junk_b):
        xs = x_sb[:, c, :]
        st = stat[:, c, :]
        act(junk_v, xs, AF.Square, [xb[c]], [junk_b, statb[c]], accum=st[:, 0:1])
        act(st[:, 1:2], st[:, 0:1], AF.Sqrt, [statb[c]], [statb[c]], scale=1.0 / D, bias=EPS)
        recip(st[:, 2:3], st[:, 1:2], [statb[c]], [statb[c]])
        act(xhat_v, xs, AF.Copy, [xb[c], statb[c]], [xhat_b], scale=st[:, 2:3])

    def norm_b(gi, xhat_v, xhat_b, dst, dst_b):
        bk = bank("all")
        pv = PSB(bk).rearrange("p (a b) -> p a b", a=8)
        for kc in range(8):
            tr(pv[:, kc, :], xhat_v[:, kc * 128:(kc + 1) * 128], ident_bf[:], [xhat_b, cb], [pb[bk]])
        tt(dst, pv, bc(gT_sb[:, gi, :].unsqueeze(2), [128, 8, 128]), ALU.mult, [pb[bk], cb], [dst_b])

    ffn_state = {}

    def ffn(fi, gi, fence_after=True):
        hT = AV(0, [128, 8, S], BF)
        if "b" not in ffn_state:
            ffn_state["b"] = dict(hTb=P.bufs(4, "ffn_hT"), w=[(P.buf("ffn_wg%d" % i), P.buf("ffn_wu%d" % i), P.buf("ffn_wd%d" % i)) for i in range(2)],
                                  act=[P.bufs(4, "ffn_act0_"), P.bufs(4, "ffn_act1_")], junk=P.buf("ffn_junk"),
                                  xh=[P.buf("ffn_xh%d" % i) for i in range(4)], sg=[P.buf("ffn_sg0"), P.buf("ffn_sg1")])
        fb = ffn_state["b"]
        hTb = fb["hTb"]
        wts = []
        for s_ in range(2):
            o = 32 + 18 * s_
            wts.append((AV(o, [128, 8, 384], BF), AV(o + 6, [128, 8, 384], BF), AV(o + 12, [128, 3, D], BF)) + fb["w"][s_])
        acts = [(AV(68, [128, 3, S], BF), fb["act"][0]), (AV(80, [128, 3, S], BF), fb["act"][1])]
        junk = AV(92, [128, D], BF)
        junk_b = fb["junk"]
        xh = [(AV(o, [128, D], BF), fb["xh"][i]) for i, o in enumerate((94, 96, 102, 104))]
        sg = [(AV(98, [128, 512], F32), fb["sg"][0]), (AV(100, [128, 512], F32), fb["sg"][1])]
        groups = [(f0, min(3, 22 - f0)) for f0 in range(0, 22, 3)]

        def load_w(gi_):
            f0, nf = groups[gi_]
            wgt, wut, wdt, bg, bu, bd = wts[gi_ % 2]
            dma("pool", wgt[:, :, 0:nf * 128], wg_d[fi][:, f0 * 128:(f0 + nf) * 128].rearrange("(kc p) f -> p kc f", p=128), [], [bg])
            dma("pool", wut[:, :, 0:nf * 128], wu_d[fi][:, f0 * 128:(f0 + nf) * 128].rearrange("(kc p) f -> p kc f", p=128), [], [bu])
            dma("pool", wdt[:, 0:nf, :], wd_d[fi][f0 * 128:(f0 + nf) * 128, :].rearrange("(j p) d -> p j d", p=128), [], [bd])

        load_w(0)

        def nA(T):
            for c in range(4 * T, 4 * T + 4):
                xv, xbuf = xh[c % 4]
                norm_a(c, xv, xbuf, junk, junk_b)

        def nB(T):
            for c in range(4 * T, 4 * T + 4):
                xv, xbuf = xh[c % 4]
                norm_b(gi, xv, xbuf, hT[:, :, c * 128:(c + 1) * 128], hTb[c // 4])

        nA(0)
        nB(0)
        nA(1)
        sgi = 0
        for gidx, (f0, nf) in enumerate(groups):
            if gidx + 1 < len(groups):
                load_w(gidx + 1)
            wgt, wut, wdt, bg, bu, bd = wts[gidx % 2]
            actv, actb = acts[gidx % 2]
            for T in range(4):
                if gidx == 0 and T >= 1:
                    nB(T)
                    if T + 1 < 4:
                        nA(T + 1)
                for j in range(nf):
                    ba = bank("all")
                    bu_ = bank("all")
                    for kc in range(8):
                        mm(PS(ba), wgt[:, kc, j * 128:(j + 1) * 128], hT[:, kc, T * 512:(T + 1) * 512], kc == 0, kc == 7,
                           [bg, hTb[T]], [pb[ba]])
                    for kc in range(8):
                        mm(PS(bu_), wut[:, kc, j * 128:(j + 1) * 128], hT[:, kc, T * 512:(T + 1) * 512], kc == 0, kc == 7,
                           [bu, hTb[T]], [pb[bu_]])
                    sgv, sgb = sg[sgi % 2]
                    sgi += 1
                    act(sgv, PS(ba), AF.Silu, [pb[ba]], [sgb])
                    tt(actv[:, j, T * 512:(T + 1) * 512], sgv, PS(bu_), ALU.mult, [sgb, pb[bu_]], [actb[T]])
            for c in range(NCH):
                for half in range(2):
                    bo = bank("all")
                    for j in range(nf):
                        mm(PS(bo), actv[:, j, c * 128:(c + 1) * 128], wdt[:, j, half * 512:(half + 1) * 512], j == 0, j == nf - 1,
                           [actb[c // 4], bd], [pb[bo]])
                    xs = x_sb[:, c, half * 512:(half + 1) * 512]
                    stt(xs, PS(bo), 0.5, xs, ALU.mult, ALU.add, [pb[bo], xb[c]], [xb[c]])
        if fence_after:
            fence()

    QM = AV(0, [96, 8, S], BF, 96)
    KE = AV(32, [96, 4, S], BF, 96)
    VT = AV(48, [128, NCH, 4, 65], BF)
    GT = AV(56.5, [128, NCH, 24], F32)
    KVC = AV(59, [128, 2, S], BF)
    KCT = AV(58, [64, 2, 128], BF, 64)
    VCA = AV(58.5, [127, 2, 98], BF, 127)
    AT = AV(88, [128, 4, S], BF)
    CT = AV(16, [128, 4, S], BF)

    def dbg_dump(name, view, r, q="sp"):
        if dbg and name in dbg_d:
            dma(q, dbg_d[name], view, r, [])

    def phase_c(bufs):
        qmb, keb, vtb, gtb, kvcb = bufs
        hTt = [(AV(67, [128, 8, 512], BF), P.buf())] * 2
        wtok = AV(75, [128, 8, 1048], BF)
        wtok_b = P.buf()
        wfeat = AV(92, [128, 8, 256], BF)
        wfeat_b = P.buf()
        sqs = [(AV(96, [128, 768], F32), P.buf()), (AV(104, [128, 768], F32), P.buf())]
        junk = AV(96, [128, D], BF)
        junk_b = sqs[0][1]
        xh = [(AV(99, [128, D], BF), P.buf()), (AV(108.5, [128, D], BF), P.buf())]
        qbfs = [(AV(101, [128, 768], BF), P.buf()), (AV(107, [128, 768], BF), P.buf())]
        rope_t = AV(102.5, [128, 4, 96], F32)
        segs = [(0, 512, 0), (768, 896, 512), (1024, 1152, 640), (896, 1024, 768), (1152, 1280, 896), (1280, 1304, 1024)]
        dma("pool", wfeat[:], win_d[:, 512:768].rearrange("(kc p) f -> p kc f", p=128), [], [wfeat_b])
        wq_b, wkv_b, wgt_b = P.buf(), P.buf(), P.buf()
        for (a_, b_, o) in segs:
            bb_ = wq_b if o == 0 else (wgt_b if o == 1024 else wkv_b)
            dma("pool", wtok[:, :, o:o + (b_ - a_)], win_d[:, a_:b_].rearrange("(kc p) f -> p kc f", p=128), [], [bb_])
        for g in range(2):
            dma("pool", KE[64:96, g, :], cd["c_efull"], [], [keb])
        memset(VT[:, :, :, 64:65], 1.0, [], [vtb])

        def norm_c(c):
            T, cc = divmod(c, 4)
            hv, hb = hTt[T % 2]
            xv, xbuf = xh[c % 2]
            norm_chunk(c, 1, xv, xbuf, junk, junk_b, hv[:, :, cc * 128:(cc + 1) * 128], hb)

        def feat(T):
            hv, hb = hTt[T % 2]
            for slot in range(2):
                bk = bank("hi")
                for kc in range(8):
                    mm(PS(bk), wfeat[:, kc, slot * 128:(slot + 1) * 128], hv[:, kc, :], kc == 0, kc == 7, [wfeat_b, hb], [pb[bk]])
                act(KVC[:, slot, T * 512:(T + 1) * 512], PS(bk), AF.Copy, [pb[bk]], [kvcb])

        def s1(c):
            T, cc = divmod(c, 4)
            hv, hb = hTt[T % 2]
            sq, sq_b = sqs[c % 2]
            qf, qf_b = sq, sq_b
            qbf, qbf_b = qbfs[c % 2]
            pr = pair()
            pq = ps[:, pr * 512:pr * 512 + 1024]
            for kc in range(8):
                mm(pq[:, 0:512], hv[:, kc, cc * 128:(cc + 1) * 128], wtok[:, kc, 0:512], kc == 0, kc == 7, [hb, wq_b], [pb[pr]])
            for kc in range(8):
                mm(pq[:, 512:1024], hv[:, kc, cc * 128:(cc + 1) * 128], wtok[:, kc, 512:1024], kc == 0, kc == 7, [hb, wkv_b], [pb[pr + 1]])
            bg_ = bank("hi")
            for kc in range(8):
                mm(PS(bg_, 24), hv[:, kc, cc * 128:(cc + 1) * 128], wtok[:, kc, 1024:1048], kc == 0, kc == 7, [hb, wgt_b], [pb[bg_]])
            prb = [pb[pr], pb[pr + 1]]
            act(sq, pq[:, 0:768], AF.Square, prb, [sq_b])
            red(S12[:, 0, :], sq.rearrange("p (h d) -> p h d", d=64), [sq_b], [s12_b])
            act(S12[:, 1, :], S12[:, 0, :], AF.Sqrt, [s12_b], [s12_b], scale=1.0 / 64, bias=EPS)
            recip(S12[:, 2, :], S12[:, 1, :], [s12_b], [s12_b])
            qf3 = qf.rearrange("p (h d) -> p h d", d=64)
            tt(qf3, pq[:, 0:768].rearrange("p (h d) -> p h d", d=64), bc(S12[:, 2, :].unsqueeze(2), [128, 12, 64]), ALU.mult,
               prb + [s12_b], [qf_b])
            act(VT[:, c, :, 0:64], pq[:, 768:1024].rearrange("p (a d) -> p a d", d=64), AF.Copy, prb, [vtb])
            act(GT[:, c, :], PS(bg_, 24), AF.Sigmoid, [pb[bg_]], [gtb])
            tt(qf, qf, gqk_sb[:], ALU.mult, [qf_b, cb], [qf_b])
            cosb = bc(cos_sb[:, c, :].unsqueeze(1), [128, 12, 8])
            sinb = bc(sin_sb[:, c, :].unsqueeze(1), [128, 12, 8])
            x1 = qf3[:, :, 0:8]
            x2 = qf3[:, :, 8:16]
            rt = rope_t.rearrange("p a (h d) -> p a h d", d=8)
            tt(rt[:, 0], x1, cosb, ALU.mult, [qf_b, cb], [rope_b])
            tt(rt[:, 1], x2, sinb, ALU.mult, [qf_b, cb], [rope_b])
            tt(rt[:, 2], x2, cosb, ALU.mult, [qf_b, cb], [rope_b])
            tt(rt[:, 3], x1, sinb, ALU.mult, [qf_b, cb], [rope_b])
            tt(x1, rt[:, 0], rt[:, 1], ALU.subtract, [rope_b], [qf_b])
            tt(x2, rt[:, 2], rt[:, 3], ALU.add, [rope_b], [qf_b])
            cp(qbf, qf, [qf_b], [qbf_b])

        def s2(c):
            qbf, qbf_b = qbfs[c % 2]
            bq = bank("hi")
            pqv = PSB(bq).rearrange("p (a b) -> p a b", a=8)
            for h in range(8):
                tr(pqv[0:64, h, :], qbf[:, h * 64:(h + 1) * 64], ident_bf[:], [qbf_b, cb], [pb[bq]])
            act(QM[0:64, :, c * 128:(c + 1) * 128], pqv[0:64], AF.Copy, [pb[bq]], [qmb])
            bk_ = bank("hi")
            pkv = PSB(bk_).rearrange("p (a b) -> p a b", a=8)
            for i in range(4):
                tr(pkv[0:64, i, :], qbf[:, 512 + i * 64:512 + (i + 1) * 64], ident_bf[:], [qbf_b, cb], [pb[bk_]])
            cp(KE[0:64, :, c * 128:(c + 1) * 128], pkv[0:64, 0:4, :], [pb[bk_]], [keb])

        for c in range(4):
            norm_c(c)
        for step in range(NCH + 1):
            if step < NCH:
                T, cc = divmod(step, 4)
                if cc == 0:
                    feat(T)
                s1(step)
                if cc == 3 and step + 1 < NCH:
                    for c2 in range(step + 1, step + 5):
                        norm_c(c2)
            if 0 <= step - 1 < NCH:
                s2(step - 1)
        fence()

    S12 = sb("s12", [128, 3, 12], F32)
    s12_b = P.buf("s12", arena=False)
    rope_b = P.buf("rope")

    def load_w1(slot, w1t, w1b):
        src = w1_d[slot].rearrange("(t d) h -> d t h", d=64)
        dma("pool", w1t[0:64], src, [], [w1b])
        dma("pool", w1t[64:128], src, [], [w1b])

    def prologue_pbias():
        w1t = AV(67, [128, 32, 256], BF)
        w1b = P.buf()
        for slot in range(2):
            load_w1(slot, w1t, w1b)
            for hc in range(2):
                bk = bank("all")
                for t in range(32):
                    mm(ps[:, bk * 512:bk * 512 + 1], w1t[0:64, t, hc * 128:(hc + 1) * 128], peT_sb[0:64, slot, t:t + 1], t == 0, t == 31,
                       [w1b, cb], [pb[bk]])
                cp(pbias_sb[:, slot, hc:hc + 1], ps[:, bk * 512:bk * 512 + 1], [pb[bk]], [cb])
        fence()

    def phase_d(bufs):
        qmb, keb, vtb, gtb, kvcb, kctb, vcab = bufs
        w1g = [AV(67, [128, 32, 256], BF), AV(88, [128, 32, 256], BF)]
        w1bs = [[P.buf(), P.buf()], [P.buf(), P.buf()]]
        hact = AV(83, [128, 2, 254], BF)
        hact_b = P.buf()
        ksq = AV(84, [128, 128], F32)
        ksq_b = P.buf()
        kcn = AV(84.5, [128, 128], F32)
        kcn_b = P.buf()
        kcb16 = AV(85, [128, 128], BF)
        kcb16_b = P.buf()
        dma("pool", VCA[:, :, 64:97], cd["c_vca"], [], [vcab])
        for th in range(2):
            memset(w1g[0][64:128, th * 16:(th + 1) * 16, :], 0.0, [], [w1bs[0][th]])
            memset(w1g[1][0:64, th * 16:(th + 1) * 16, :], 0.0, [], [w1bs[1][th]])
        for slot in range(2):
            src = w1_d[slot].rearrange("(t d) h -> d t h", d=64)
            for th in range(2):
                tsl = slice(th * 16, (th + 1) * 16)
                dma("pool", w1g[0][0:64, tsl, :], src[:, tsl, :], [], [w1bs[0][th]])
                dma("sp", w1g[1][64:128, tsl, :], w1g[0][0:64, tsl, :], [w1bs[0][th]], [w1bs[1][th]])
            bks = {(hc, g): bank("all") for hc in range(2) for g in range(2)}
            for th in range(2):
                for hc in range(2):
                    for g in range(2):
                        bk = bks[(hc, g)]
                        for t in range(th * 16, (th + 1) * 16):
                            mm(ps[:, bk * 512:bk * 512 + 127], w1g[g][:, t, hc * 128:(hc + 1) * 128],
                               KVC[:, slot, t:t + 16 * 126 + 1:16], t == 0, t == 31, [w1bs[g][th], kvcb], [pb[bk]])
            for hc in range(2):
                for g in range(2):
                    bk = bks[(hc, g)]
                    act(hact[:, hc, g * 127:(g + 1) * 127], ps[:, bk * 512:bk * 512 + 127], AF.Silu, [pb[bk], cb], [hact_b],
                        bias=pbias_sb[:, slot, hc:hc + 1])
            bk = bank("all")
            for g in range(2):
                for hc in range(2):
                    mm(ps[0:127, bk * 512 + g * 64:bk * 512 + (g + 1) * 64], hact[:, hc, g * 127:(g + 1) * 127], w2_sb[:, slot, hc, :],
                       hc == 0, hc == 1, [hact_b, cb], [pb[bk]])
            pk = ps[0:127, bk * 512:bk * 512 + 128]
            if slot == 0:
                act(ksq[0:127], pk, AF.Square, [pb[bk]], [ksq_b])
                red(S12[0:127, 0, 0:2], ksq[0:127].rearrange("p (g d) -> p g d", d=64), [ksq_b], [s12_b])
                act(S12[0:127, 1, 0:2], S12[0:127, 0, 0:2], AF.Sqrt, [s12_b], [s12_b], scale=1.0 / 64, bias=EPS)
                recip(S12[0:127, 2, 0:2], S12[0:127, 1, 0:2], [s12_b], [s12_b])
                tt(kcn[0:127].rearrange("p (g d) -> p g d", d=64), pk.rearrange("p (g d) -> p g d", d=64),
                   bc(S12[0:127, 2, 0:2].unsqueeze(2), [127, 2, 64]), ALU.mult, [pb[bk], s12_b], [kcn_b])
                tt(kcb16[0:127], kcn[0:127], gkc_sb[0:127], ALU.mult, [kcn_b, cb], [kcb16_b])
                bt = bank("all")
                ptv = PSB(bt).rearrange("p (a b) -> p a b", a=8)
                for g in range(2):
                    tr(ptv[0:64, g, 0:127], kcb16[0:127, g * 64:(g + 1) * 64], ident_bf[0:127, 0:127], [kcb16_b, cb], [pb[bt]])
                act(KCT[0:64, :, 0:127], ptv[0:64, 0:2, 0:127], AF.Copy, [pb[bt]], [kctb])
            else:
                act(VCA[:, :, 0:64], pk.rearrange("p (g d) -> p g d", d=64), AF.Copy, [pb[bk]], [vcab])
        fence()

    def phase_ef(bufs):
        qmb, keb, vtb, gtb, kvcb, kctb, vcab, atb = bufs
        qmask_b = P.bufs(4, "qmask")
        ptc = AV(59, [127, 4, 512], BF, 127)
        ptc_b = P.bufs(4, "ptc")
        NPT = 5
        LOOK = 4
        pts = [(AV(o, [128, 512], BF), P.buf()) for o in (63, 64, 65, 66, 114)]
        otsb = [(AV(67 + 2 * i, [65, 512], F32, 65), P.buf()) for i in range(2)]
        rank = AV(71, [128, 32, 32], F32)
        rank_b = P.buf()
        sm = AV(75, [128, 512], F32)
        rs4 = sm[:, 0:4]
        ri4 = sm[:, 8:12]
        cfc = sm[:, 16:20]
        impw = sm[:, 32:160]
        imp = sm[:, 288:320]
        cnt = sm[:, 352:384]
        r4 = sm[:, 416:420]
        cf4 = sm[:, 420:424]
        sm_b = P.buf()
        sm2_b = P.buf()
        sm3_b = P.buf()
        cnt2 = AV(77.5, [128, 32], F32)
        cnt_b = P.buf()
        negm = AV(77, [128, 32], BF)
        negm_b = P.buf()
        tmpo = AV(78.5, [128, 4, 64], F32)
        tmpo_b = P.buf()
        atoks = [(AV(80, [128, 4, 512], F32), P.bufs(4, "atok0")), (AV(106, [128, 4, 512], F32), P.bufs(4, "atok1"))]
        abfs = [(AV(104 + i, [128, 512], BF), P.buf()) for i in range(2)]
        ctr = {"pt": 0, "ot": 0, "s": 0, "o": 0, "t": 0}

        def sbank():
            ctr["s"] += 1
            return ctr["s"] % 4

        def obank():
            ctr["o"] += 1
            return 4 + ctr["o"] % 2

        def tbank():
            ctr["t"] += 1
            return 6 + ctr["t"] % 2

        def e_tasks(T):
            atok, atok_b = atoks[T % 2]
            tasks = []
            for g in range(2):
                def t_exp(g=g):
                    for hh in range(4):
                        h = g * 4 + hh
                        bs = sbank()
                        mm(ps[0:127, bs * 512:(bs + 1) * 512], KCT[0:64, g, 0:127], QM[0:64, h, T * 512:(T + 1) * 512], True, True, [kctb, qmb], [pb[bs]])
                        act(ptc[:, hh, :], ps[0:127, bs * 512:(bs + 1) * 512], AF.Exp, [pb[bs]], [ptc_b[hh]], scale=0.125)
                        tt(ptc[:, hh, :], ptc[:, hh, :], cmask[:, T, :], ALU.mult, [ptc_b[hh], cb], [ptc_b[hh]], eng="pool")
                tasks.append(t_exp)
                for cc in range(4):
                    def t_cc_a(g=g, cc=cc):
                        c = 4 * T + cc
                        bp = tbank()
                        for hh in range(4):
                            o = bp * 512 + hh * 97
                            mm(ps[:, o:o + 97], ptc[:, hh, cc * 128:(cc + 1) * 128], VCA[:, g, 0:97], True, True, [ptc_b[hh], vcab], [pb[bp]])
                        pc = ps[:, bp * 512:bp * 512 + 388].rearrange("p (h x) -> p h x", x=97)
                        prb = [pb[bp]]
                        ts(rs4.unsqueeze(2), pc[:, :, 64:65], 1e-30, ALU.max, prb, [sm_b])
                        recip(ri4, rs4, [sm_b], [sm_b])
                        tt(cfc, ri4, GT[:, c, g * 4:(g + 1) * 4], ALU.mult, [sm_b, gtb], [sm_b])
                        tt(atok[:, cc, g * 256:(g + 1) * 256].rearrange("p (h d) -> p h d", d=64), pc[:, :, 0:64],
                           bc(cfc.unsqueeze(2), [128, 4, 64]), ALU.mult, prb + [sm_b], [atok_b[cc]])
                        tt(impw.rearrange("p (h j) -> p h j", j=32), pc[:, :, 65:97], bc(ri4.unsqueeze(2), [128, 4, 32]), ALU.mult,
                           prb + [sm_b], [sm2_b])
                        red(imp, impw.rearrange("p (h j) -> p j h", j=32), [sm2_b], [sm2_b])
                        tt(imp, imp, m1_sb[:, c, :], ALU.mult, [sm2_b, cb], [sm2_b])
                        tt(imp, imp, a1_sb[:, c, :], ALU.add, [sm2_b, cb], [sm2_b])
                        tt(rank, bc(imp.unsqueeze(1), [128, 32, 32]), bc(imp.unsqueeze(2), [128, 32, 32]), ALU.is_gt, [sm2_b], [rank_b])
                        red(cnt2, rank, [rank_b], [cnt_b])
                        ts(negm, cnt2, 16.0, ALU.is_ge, [cnt_b], [negm_b], s2=NEG, op1=ALU.mult)

                    def t_cc_b(g=g, cc=cc):
                        c = 4 * T + cc
                        bt = tbank()
                        ptv = PSB(bt).rearrange("p (a b) -> p a b", a=8)
                        tr(ptv[0:32, 0, :], negm[:, :], ident_bf[:], [negm_b, cb], [pb[bt]])
                        cp(QM[64:96, g * 4:(g + 1) * 4, c * 128:(c + 1) * 128], bc(ptv[0:32, 0:1, :], [32, 4, 128]), [pb[bt]], [qmask_b[T]])
                    tasks.append(t_cc_a)
                    tasks.append(t_cc_b)
            return tasks

        def f_items(T):
            items = []
            for br in (1, 0):
                for h in range(8):
                    g = h // 4
                    lst = []
                    if br == 0:
                        for kc in range(4 * T + 4):
                            if kc < 4 * T:
                                lst.append((kc, 0, 512, None, None))
                            else:
                                i = kc - 4 * T
                                lst.append((kc, 128 * i, 512, 128 * i, tric))
                    else:
                        for kc in range(max(0, 4 * T - 4), 4 * T + 4):
                            if kc < 4 * T:
                                i = kc - (4 * T - 4)
                                lst.append((kc, 0, 128 * (i + 1), 128 * i, triw))
                            else:
                                i = kc - 4 * T
                                lst.append((kc, 128 * i, 512, 128 * i, tric))
                    for n_i, it in enumerate(lst):
                        items.append((br, h, g, n_i == 0, n_i == len(lst) - 1) + it)
            return items

        def run_f(T, extra_tasks):
            atok, atok_b = atoks[T % 2]
            items = f_items(T)
            n = len(items)
            state = {}
            posts = []
            every = max(1, n // (len(extra_tasks) + 1)) if extra_tasks else n + 1
            extra = list(extra_tasks)
            cur_o = {}
            for idx in range(n + LOOK):
                if idx < n:
                    br, h, g, first, last, kc, c0, c1, m0, mk = items[idx]
                    K = 96 if br == 0 else 64
                    kidx = g if br == 0 else 2 + g
                    bs = sbank()
                    rd = [keb, qmb] + ([qmask_b[T]] if br == 0 else [])
                    mm(ps[:, bs * 512 + c0:bs * 512 + c1], KE[0:K, kidx, kc * 128:(kc + 1) * 128], QM[0:K, h, T * 512 + c0:T * 512 + c1],
                       True, True, rd, [pb[bs]])
                    ptv_, ptb_ = pts[ctr["pt"] % NPT]
                    ctr["pt"] += 1
                    act(ptv_[:, c0:c1], ps[:, bs * 512 + c0:bs * 512 + c1], AF.Exp, [pb[bs]], [ptb_], scale=0.125)
                    if mk is not None:
                        mk01 = tric01 if mk is tric else triw01
                        tt(ptv_[:, m0:m0 + 128], ptv_[:, m0:m0 + 128], mk01[:], ALU.mult, [ptb_, cb], [ptb_], eng="pool")
                    state[idx] = (ptv_, ptb_)
                j = idx - LOOK
                if j >= 0:
                    br, h, g, first, last, kc, c0, c1, m0, mk = items[j]
                    ptv_, ptb_ = state.pop(j)
                    if first:
                        cur_o[(br, h)] = obank()
                    bo = cur_o[(br, h)]
                    vidx = br * 2 + g
                    mm(ps[0:65, bo * 512 + c0:bo * 512 + c1], VT[:, kc, vidx, 0:65], ptv_[:, c0:c1], first, last, [vtb, ptb_], [pb[bo]])
                    if last:
                        ov, ob = otsb[ctr["ot"] % 2]
                        ctr["ot"] += 1
                        cp(ov, ps[0:65, bo * 512:(bo + 1) * 512], [pb[bo]], [ob])

                        def post(br=br, h=h, ov=ov, ob=ob):
                            bt = tbank()
                            tov = ps[:, bt * 512:bt * 512 + 260].rearrange("p (a x) -> p a x", x=65)
                            for cc in range(4):
                                tr(tov[:, cc, :], ov[0:65, cc * 128:(cc + 1) * 128], ident_f[0:65, 0:65], [ob, cb], [pb[bt]])
                            recip(r4.unsqueeze(2), tov[:, :, 64:65], [pb[bt]], [sm3_b])
                            tt(cf4, r4, GT[:, 4 * T:4 * T + 4, 8 + br * 8 + h], ALU.mult, [sm3_b, gtb], [sm3_b])
                            tt(tmpo, tov[:, :, 0:64], bc(cf4.unsqueeze(2), [128, 4, 64]), ALU.mult, [pb[bt], sm3_b], [tmpo_b])
                            av = atok[:, :, h * 64:(h + 1) * 64]
                            tt(av, av, tmpo, ALU.add, [tmpo_b] + atok_b, atok_b)
                        posts.append((idx + 4, post))
                while posts and posts[0][0] <= idx:
                    posts.pop(0)[1]()
                if extra and idx % every == every - 1:
                    extra.pop(0)()
            for _, p_ in posts:
                p_()
            for t_ in extra:
                t_()
            for cc in range(4):
                c = 4 * T + cc
                abv, abb = abfs[cc % 2]
                cp(abv, atok[:, cc, :], atok_b, [abb])
                bt = tbank()
                ptv = PSB(bt).rearrange("p (a b) -> p a b", a=8)
                for j_ in range(4):
                    tr(ptv[:, j_, :], abv[:, j_ * 128:(j_ + 1) * 128], ident_bf[:], [abb, cb], [pb[bt]])
                act(AT[:, :, c * 128:(c + 1) * 128], ptv[:, 0:4, :], AF.Copy, [pb[bt]], [atb])

        for t_ in e_tasks(0):
            t_()
        for T in range(4):
            run_f(T, e_tasks(T + 1) if T < 3 else [])
        fence()

    def phase_h(bufs):
        atb, ctb, hTt = bufs
        wcv = AV(32, [128, 8, 1536], BF)
        wcv_b = P.buf()
        junk = AV(81, [128, D], BF)
        junk_b = P.buf()
        xh = [(AV(83, [128, D], BF), P.buf()), (AV(106, [128, D], BF), P.buf())]
        usb = AV(85, [128, 512], F32)
        usb_b = P.buf()
        cu = AV(56, [128, 4, 516], F32)
        cu_b = P.bufs(4, "cu")
        acc = AV(104, [128, 512], F32)
        acc_b = P.buf()
        tap = AV(108, [128, 512], F32)
        tap_b = P.buf()
        wcv_fb = P.bufs(4, "wcvf")
        for fc in range(4):
            for sel in range(3):
                o = sel * 512 + fc * 128
                dma("pool", wcv[:, :, o:o + 128], win_d[:, 1304 + o:1304 + o + 128].rearrange("(kc p) f -> p kc f", p=128), [], [wcv_fb[fc]])
        for fc in range(4):
            memset(cu[:, fc, 0:2], 0.0, [], [cu_b[fc]])
        def normT(T):
            hv, hb = hTt[T]
            for cc in range(4):
                c = 4 * T + cc
                xv, xbuf = xh[c % 2]
                norm_chunk(c, 1, xv, xbuf, junk, junk_b, hv[:, :, cc * 128:(cc + 1) * 128], hb)

        normT(0)
        for T in range(4):
            hv, hb = hTt[T]
            for fc in range(4):
                if fc == 1 and T + 1 < 4:
                    normT(T + 1)
                bks = []
                for sel in range(3):
                    bk = bank("all")
                    bks.append(bk)
                    for kc in range(8):
                        mm(PS(bk), wcv[:, kc, sel * 512 + fc * 128:sel * 512 + (fc + 1) * 128], hv[:, kc, :], kc == 0, kc == 7,
                           [wcv_fb[fc], hb], [pb[bk]])
                bB, bC, bU = bks
                act(usb, PS(bU), AF.Copy, [pb[bU]], [usb_b])
                tt(cu[:, fc, 2:514], PS(bC), usb, ALU.mult, [pb[bC], usb_b], [cu_b[fc]])
                ts(acc, cu[:, fc, 2:514], convw_sb[:, fc, 2:3], ALU.mult, [cu_b[fc], cb], [acc_b])
                stt(acc, cu[:, fc, 1:513], convw_sb[:, fc, 1:2], acc, ALU.mult, ALU.add, [cu_b[fc], cb, acc_b], [acc_b])
                stt(acc, cu[:, fc, 0:512], convw_sb[:, fc, 0:1], acc, ALU.mult, ALU.add, [cu_b[fc], cb, acc_b], [acc_b])
                tt(CT[:, fc, T * 512:(T + 1) * 512], acc, PS(bB), ALU.mult, [acc_b, pb[bB]], [ctb])
                cp(cu[:, fc, 0:2], cu[:, fc, 512:514], [cu_b[fc]], [cu_b[fc]])
        fence()

    def phase_i(bufs):
        atb, ctb, hTt = bufs
        wga = AV(32, [128, 8, 512], BF)
        wgc = AV(40, [128, 8, 512], BF)
        wa = AV(48, [128, 4, 512], BF)
        wc = AV(52, [128, 4, 512], BF)
        wo = AV(56, [128, 4, D], BF)
        wga_b, wgc_b, wa_b, wc_b, wo_b = P.buf(), P.buf(), P.buf(), P.buf(), P.buf()
        sgas = [(AV(84, [128, 512], F32), P.buf()), (AV(104, [128, 512], F32), P.buf())]
        sgcs = [(AV(86, [128, 512], F32), P.buf()), (AV(106, [128, 512], F32), P.buf())]
        mbs = [(AV(108, [128, 512], BF), P.buf()), (AV(109, [128, 512], BF), P.buf())]
        mTs = [(AV(110, [128, 4, 128], BF), P.buf()), (AV(111, [128, 4, 128], BF), P.buf())]
        for half in range(2):
            hs = slice(half * 512, (half + 1) * 512)
            dma("pool", wga[:], win_d[:, 2840 + half * 512:2840 + (half + 1) * 512].rearrange("(kc p) f -> p kc f", p=128), [], [wga_b])
            dma("pool", wgc[:], win_d[:, 3864 + half * 512:3864 + (half + 1) * 512].rearrange("(kc p) f -> p kc f", p=128), [], [wgc_b])
            dma("pool", wa[:], wa_d[:, hs].rearrange("(kc p) f -> p kc f", p=128), [], [wa_b])
            dma("pool", wc[:], wc_d[:, hs].rearrange("(kc p) f -> p kc f", p=128), [], [wc_b])
            dma("pool", wo[:], wo_d[half * 512:(half + 1) * 512, :].rearrange("(kc p) f -> p kc f", p=128), [], [wo_b])
            def s1(c, half=half):
                T, cc = divmod(c, 4)
                hv, hb = hTt[T]
                sga, sga_b = sgas[c % 2]
                sgc, sgc_b = sgcs[c % 2]
                mb, mb_b = mbs[c % 2]
                bga, bgc, baw, bcw = bank("all"), bank("all"), bank("all"), bank("all")
                for kc in range(8):
                    mm(PS(bga), hv[:, kc, cc * 128:(cc + 1) * 128], wga[:, kc, :], kc == 0, kc == 7, [hb, wga_b], [pb[bga]])
                for kc in range(8):
                    mm(PS(bgc), hv[:, kc, cc * 128:(cc + 1) * 128], wgc[:, kc, :], kc == 0, kc == 7, [hb, wgc_b], [pb[bgc]])
                for j in range(4):
                    mm(PS(baw), AT[:, j, c * 128:(c + 1) * 128], wa[:, j, :], j == 0, j == 3, [atb, wa_b], [pb[baw]])
                for j in range(4):
                    mm(PS(bcw), CT[:, j, c * 128:(c + 1) * 128], wc[:, j, :], j == 0, j == 3, [ctb, wc_b], [pb[bcw]])
                act(sga, PS(bga), AF.Sigmoid, [pb[bga]], [sga_b])
                act(sgc, PS(bgc), AF.Sigmoid, [pb[bgc]], [sgc_b])
                tt(sga, sga, PS(baw), ALU.mult, [sga_b, pb[baw]], [sga_b])
                tt(sgc, sgc, PS(bcw), ALU.mult, [sgc_b, pb[bcw]], [sgc_b])
                tt(mb, sga, sgc, ALU.add, [sga_b, sgc_b], [mb_b])

            def s2(c):
                mb, mb_b = mbs[c % 2]
                mT, mT_b = mTs[c % 2]
                bt = bank("all")
                ptv = PSB(bt).rearrange("p (a b) -> p a b", a=8)
                for j in range(4):
                    tr(ptv[:, j, :], mb[:, j * 128:(j + 1) * 128], ident_bf[:], [mb_b, cb], [pb[bt]])
                act(mT, ptv[:, 0:4, :], AF.Copy, [pb[bt]], [mT_b])

            def s3(c):
                mT, mT_b = mTs[c % 2]
                for nh in range(2):
                    bo = bank("all")
                    for j in range(4):
                        mm(PS(bo), mT[:, j, :], wo[:, j, nh * 512:(nh + 1) * 512], j == 0, j == 3, [mT_b, wo_b], [pb[bo]])
                    xs = x_sb[:, c, nh * 512:(nh + 1) * 512]
                    tt(xs, xs, PS(bo), ALU.add, [xb[c], pb[bo]], [xb[c]])

            for step in range(NCH + 2):
                if step < NCH:
                    s1(step)
                if 0 <= step - 2 < NCH:
                    s3(step - 2)
                if 0 <= step - 1 < NCH:
                    s2(step - 1)
        fence()

    prologue_pbias()
    qmb, keb, vtb, gtb, kvcb, kctb, vcab, atb, ctb = (P.buf("QM"), P.buf("KE"), P.buf("VT"), P.buf("GT"), P.buf("KVC"),
                                                     P.buf("KCT"), P.buf("VCA"), P.buf("AT"), P.buf("CT"))
    mix_hTt = [(AV(o, [128, 8, 512], BF), P.buf("mixhT%d" % i)) for i, o in enumerate((0, 8, 65, 73))]
    for s in range(nseq):
        for c in range(NCH):
            dma("sp", x_sb[:, c, :], x_d[s, c * 128:(c + 1) * 128, :], [], [xb[c]])
        if "A" in phases:
            ffn(0, 0)
        if dbg and s == 0:
            dma("sp", dbg_d["d_x1"].rearrange("(c p) d -> p c d", p=128), x_sb[:], list(xb), [])
        if "C" in phases:
            phase_c((qmb, keb, vtb, gtb, kvcb))
        if dbg and s == 0 and "C" in phases:
            dma("pool", dbg_d["d_kvc"], KVC, [kvcb], [])
            dma("sp", dbg_d["d_gates"], GT, [gtb], [])
            fence()
        if "D" in phases:
            phase_d((qmb, keb, vtb, gtb, kvcb, kctb, vcab))
        if dbg and s == 0 and "D" in phases:
            dma("pool", dbg_d["d_kct"], KCT[:, :, 0:127], [kctb], [])
            dma("pool", dbg_d["d_vca"], VCA[:, :, 0:97], [vcab], [])
            fence()
        if "E" in phases:
            phase_ef((qmb, keb, vtb, gtb, kvcb, kctb, vcab, atb))
        if dbg and s == 0 and "C" in phases:
            dma("pool", dbg_d["d_qm"], QM, [qmb], [])
            dma("pool", dbg_d["d_ke"], KE, [keb], [])
            dma("pool", dbg_d["d_v"], VT, [vtb], [])
            fence()
        if dbg and s == 0 and "E" in phases:
            dma("pool", dbg_d["d_aT"], AT, [atb], [])
            fence()
        if "H" in phases:
            phase_h((atb, ctb, mix_hTt))
        if dbg and s == 0 and "H" in phases:
            dma("pool", dbg_d["d_cT"], CT, [ctb], [])
            fence()
        if "I" in phases:
            phase_i((atb, ctb, mix_hTt))
        if dbg and s == 0:
            dma("sp", dbg_d["d_x2"].rearrange("(c p) d -> p c d", p=128), x_sb[:], list(xb), [])
        if "J" in phases:
            ffn(1, 2, fence_after=not (s + 1 < nseq and "A" in phases))
        for c in range(NCH):
            dma("sp", out_d[s, c * 128:(c + 1) * 128, :], x_sb[:, c, :], [xb[c]], [])
    P.emit()
    es.close()
    return nc, P


def make_in_maps(inp, nseq, n_cores):
    f = lambda a: np.ascontiguousarray(np.asarray(a, dtype=np.float32))
    x = f(inp["x"])
    shared = {
        "w_gate1": f(inp["ffn1_w_gate"][0]), "w_up1": f(inp["ffn1_w_up"][0]), "w_down1": f(inp["ffn1_w_down"][0]),
        "w_gate2": f(inp["ffn2_w_gate"][0]), "w_up2": f(inp["ffn2_w_up"][0]), "w_down2": f(inp["ffn2_w_down"][0]),
        "w_in": f(inp["w_in"][0]),
        "cmp_k_w1": f(inp["cmp_k_w1"][0]), "cmp_v_w1": f(inp["cmp_v_w1"][0]),
        "cmp_k_w2": f(inp["cmp_k_w2"][0]), "cmp_v_w2": f(inp["cmp_v_w2"][0]),
        "w_a": f(inp["w_attn_branch"][0]), "w_c": f(inp["w_conv_branch"][0]), "w_o": f(inp["w_out"][0]),
    }
    g3 = np.stack([f(inp["ffn1_norm_g"][0]), f(inp["mix_norm_g"][0]), f(inp["ffn2_norm_g"][0])], 0)
    shared["gT"] = np.ascontiguousarray(g3.reshape(3, 8, 128).transpose(2, 0, 1))
    qg = f(inp["q_norm_g"][0])
    kg = f(inp["k_norm_g"][0])
    gqk = np.concatenate([np.tile(qg, 8), np.tile(kg[1], 2), np.tile(kg[2], 2)])
    shared["gqk"] = np.ascontiguousarray(np.broadcast_to(gqk[None, :], (128, 768)))
    shared["gkc"] = np.ascontiguousarray(np.broadcast_to(np.tile(kg[0], 2)[None, :], (128, 128)))
    pek = f(inp["cmp_pe_k"][0])
    pev = f(inp["cmp_pe_v"][0])
    shared["peT"] = np.ascontiguousarray(np.stack([pek.T, pev.T], 1))
    cw = f(inp["conv_w"][0])
    shared["convw"] = np.ascontiguousarray(cw.reshape(3, 4, 128).transpose(2, 1, 0))
    shared.update(_const_tables())
    maps = []
    for i in range(n_cores):
        m = dict(shared)
        m["x"] = np.ascontiguousarray(x[i * nseq:(i + 1) * nseq])
        maps.append(m)
    return maps


_CACHE = {}


def kernel(**inputs):
    nseq = 4
    if "nc" not in _CACHE:
        _CACHE["nc"] = build(nseq)[0]
    nc = _CACHE["nc"]
    maps = make_in_maps(inputs, nseq, N_CORES)
    res = run_bass_kernel_spmd(nc, maps, core_ids=list(range(N_CORES)))
    out = np.concatenate([np.asarray(r["out"]) for r in res.results], axis=0)
    return out.astype(np.float32)
```

```python
import numpy as np
from contextlib import ExitStack
import concourse.bass as bass
import concourse.mybir as mybir
from concourse.bass_utils import run_bass_kernel_spmd

F32 = mybir.dt.float32
BF = mybir.dt.bfloat16
AF = mybir.ActivationFunctionType
ALU = mybir.AluOpType
AX = mybir.AxisListType

S = 2048
D = 1024
DFF = 2816
NCH = 16
NEG = -30000.0
EPS = 1e-6
N_CORES = 8

ENG_ATTR = {"pe": "tensor", "act": "scalar", "dve": "vector", "pool": "gpsimd", "sp": "sync"}
N_DMA_SEMS = 16
EPOCH = 20000


class Buf:
    __slots__ = ("name", "lw", "rd")

    def __init__(self, name):
        self.name = name
        self.lw = None
        self.rd = []


class Op:
    __slots__ = ("eng", "fn", "deps", "dma", "sig", "sem", "val", "idx", "qpos")

    def __init__(self, eng, fn, dma):
        self.eng = eng
        self.fn = fn
        self.deps = []
        self.dma = dma
        self.sig = False
        self.sem = None
        self.val = 0
        self.idx = 0
        self.qpos = 0


class Prog:
    def __init__(self, nc, es):
        self.nc = nc
        self.es = es
        self.ops = []
        self.nbuf = 0
        self.allbufs = []
        self.last_fence = None

    def buf(self, name=None, arena=True):
        self.nbuf += 1
        b = Buf(name or f"b{self.nbuf}")
        if arena:
            self.allbufs.append(b)
            b.lw = self.last_fence
        return b

    def bufs(self, n, name="b", arena=True):
        return [self.buf(f"{name}{i}", arena) for i in range(n)]

    def add(self, eng, fn, r=(), w=(), dma=False):
        op = Op(eng, fn, dma)
        op.idx = len(self.ops)
        deps = set()
        for b in r:
            if b.lw is not None:
                deps.add(b.lw)
        for b in w:
            if b.lw is not None:
                deps.add(b.lw)
            for x in b.rd:
                deps.add(x)
        deps.discard(op.idx)
        op.deps = sorted(deps)
        for b in r:
            b.rd.append(op.idx)
        for b in w:
            b.lw = op.idx
            b.rd = []
        self.ops.append(op)
        return op

    def dma(self, q, fn, r=(), w=()):
        return self.add(q, fn, r, w, dma=True)

    def emit(self):
        nc = self.nc
        ops = self.ops
        cnt = {}
        for op in ops:
            op.qpos = cnt.get(op.eng, 0)
            cnt[op.eng] = op.qpos + 1
        for op in ops:
            keep = []
            for d in op.deps:
                p = ops[d]
                if not p.dma and p.eng == op.eng and not op.dma:
                    if p.eng == "pe":
                        continue
                    if p.eng != "pool" and op.qpos - p.qpos > 3:
                        continue
                p.sig = True
                keep.append(d)
            op.deps = keep
        sem_ctr = [0]

        def new_sem(tag):
            sem_ctr[0] += 1
            return self.es.enter_context(nc.semaphore(f"{tag}{sem_ctr[0]}"))

        eng_sem = {}
        eng_cnt = {}
        dq = {}
        all_dma_last = []
        for op in ops:
            if op.dma:
                if op.eng not in dq:
                    dq[op.eng] = {"sems": [new_sem("dq" + op.eng) for _ in range(N_DMA_SEMS)],
                                  "val": [0] * N_DMA_SEMS, "prev": [None] * N_DMA_SEMS, "n": 0}
                q = dq[op.eng]
                s = q["n"] % N_DMA_SEMS
                q["n"] += 1
                if q["prev"][s] is not None:
                    op.deps.append(q["prev"][s])
                q["val"][s] += 16
                op.sem = q["sems"][s]
                op.val = q["val"][s]
                op.sig = True
                q["prev"][s] = op.idx
            elif op.sig:
                e = op.eng
                if e not in eng_sem or eng_cnt[e] >= EPOCH:
                    eng_sem[e] = new_sem(e)
                    eng_cnt[e] = 0
                eng_cnt[e] += 1
                op.sem = eng_sem[e]
                op.val = eng_cnt[e]
        self.n_sems = sem_ctr[0]
        last_dma = [p for q in dq.values() for p in q["prev"] if p is not None]
        by_eng = {}
        for op in ops:
            by_eng.setdefault(op.eng, []).append(op)
        assert "sp" in by_eng
        block = self.es.enter_context(nc.Block())

        def make_section(lst, is_last_waiter):
            def section(e):
                known = {}
                for op in lst:
                    for d in op.deps:
                        p = ops[d]
                        k = id(p.sem)
                        if known.get(k, 0) >= p.val:
                            continue
                        e.wait_ge(p.sem, p.val)
                        known[k] = p.val
                    ins = op.fn(e)
                    if op.sig:
                        ins.then_inc(op.sem, 16 if op.dma else 1)
                if is_last_waiter:
                    for d in last_dma:
                        p = ops[d]
                        e.wait_ge(p.sem, p.val)
            return section

        for ename, lst in by_eng.items():
            getattr(block, ENG_ATTR[ename])(make_section(lst, ename == "sp"))


def _const_tables():
    c = {}
    pos = np.arange(S)
    inv_freq = (500000.0 ** (-np.arange(0, 16, 2, dtype=np.float32) / 16)).astype(np.float32)
    ang = pos.astype(np.float32)[:, None] * inv_freq[None, :]
    cs = np.cos(ang).astype(np.float32).reshape(NCH, 128, 8).transpose(1, 0, 2)
    sn = np.sin(ang).astype(np.float32).reshape(NCH, 128, 8).transpose(1, 0, 2)
    c["c_cos"] = np.ascontiguousarray(cs)
    c["c_sin"] = np.ascontiguousarray(sn)
    k = np.arange(128)[:, None]
    q = np.arange(128)[None, :]
    c["c_tric"] = np.where(k <= q, 0.0, NEG).astype(np.float32)
    c["c_triw"] = np.where(k > q, 0.0, NEG).astype(np.float32)
    c["c_tric01"] = (k <= q).astype(np.float32)
    c["c_triw01"] = (k > q).astype(np.float32)
    n = np.arange(127)[:, None, None]
    T = np.arange(4)[None, :, None]
    ql = np.arange(512)[None, None, :]
    c["c_cmask"] = (16 * n + 31 <= 512 * T + ql).astype(np.float32)
    qq = (np.arange(NCH)[None, :, None] * 128 + np.arange(128)[:, None, None])
    j = np.arange(32)[None, None, :]
    cur = qq // 64
    f0 = (j == 0)
    f1 = (j == cur)
    f2 = (j == cur - 1)
    forced = f0 | f1 | f2
    future = (64 * j > qq)
    c["c_m1"] = np.where(forced | future, 0.0, 1.0).astype(np.float32)
    a1 = np.where(future, -(1.0 + j), 0.0)
    a1 = np.where(f2, 3.0e4, a1)
    a1 = np.where(f1, 2.0e4, a1)
    a1 = np.where(f0, 1.0e4, a1)
    c["c_a1"] = a1.astype(np.float32)
    nn = np.arange(127)[:, None]
    jj = np.arange(32)[None, :]
    ovl = ((16 * nn < 64 * jj + 64) & (16 * nn + 32 > 64 * jj)).astype(np.float32)
    va = np.concatenate([np.ones((127, 1), np.float32), ovl], axis=1)
    c["c_vca"] = np.ascontiguousarray(np.broadcast_to(va[:, None, :], (127, 2, 33))).astype(np.float32)
    e = (np.arange(S)[None, :] // 64 == np.arange(32)[:, None]).astype(np.float32)
    c["c_efull"] = e
    c["c_ident"] = np.eye(128, dtype=np.float32)
    return c


def build(nseq=4, dbg=False, phases="ABCDEFHIJ"):
    nc = bass.Bass("TRN2", target_bir_lowering=False)
    es = ExitStack()
    P = Prog(nc, es)

    def din(name, shape):
        return nc.dram_tensor(name, list(shape), F32, kind="ExternalInput").ap()

    x_d = din("x", [nseq, S, D])
    out_d = nc.dram_tensor("out", [nseq, S, D], F32, kind="ExternalOutput").ap()
    wg_d = [din("w_gate1", [D, DFF]), din("w_gate2", [D, DFF])]
    wu_d = [din("w_up1", [D, DFF]), din("w_up2", [D, DFF])]
    wd_d = [din("w_down1", [DFF, D]), din("w_down2", [DFF, D])]
    win_d = din("w_in", [D, 4888])
    w1_d = [din("cmp_k_w1", [2048, 256]), din("cmp_v_w1", [2048, 256])]
    w2_d = [din("cmp_k_w2", [256, 64]), din("cmp_v_w2", [256, 64])]
    wa_d = din("w_a", [512, D])
    wc_d = din("w_c", [512, D])
    wo_d = din("w_o", [D, D])
    gT_d = din("gT", [128, 3, 8])
    gqk_d = din("gqk", [128, 768])
    gkc_d = din("gkc", [128, 128])
    peT_d = din("peT", [64, 2, 32])
    convw_d = din("convw", [128, 4, 3])
    cd = {k: din(k, v.shape) for k, v in _const_tables().items()}
    dbg_d = {}
    if dbg:
        for nm, shp in [("d_x1", [S, D]), ("d_qm", [96, 8, S]), ("d_ke", [96, 4, S]), ("d_v", [128, 16, 4, 65]),
                        ("d_kct", [64, 2, 127]), ("d_vca", [127, 2, 97]), ("d_gates", [128, 16, 24]),
                        ("d_aT", [128, 4, S]), ("d_cT", [128, 4, S]), ("d_x2", [S, D]), ("d_kvc", [128, 2, S])]:
            dbg_d[nm] = nc.dram_tensor(nm, shp, F32, kind="ExternalOutput").ap()

    def sb(name, shape, dt):
        return es.enter_context(nc.sbuf_tensor("sb_" + name, list(shape), dt))

    x_sb = sb("x_sb", [128, NCH, D], F32)
    xb = P.bufs(NCH, "x", arena=False)
    ident_bf = sb("ident_bf", [128, 128], BF)
    ident_f = sb("ident_f", [128, 128], F32)
    tric = sb("tric", [128, 128], BF)
    triw = sb("triw", [128, 128], BF)
    tric01 = sb("tric01", [128, 128], BF)
    triw01 = sb("triw01", [128, 128], BF)
    cmask = sb("cmask", [127, 4, 512], BF)
    m1_sb = sb("m1", [128, NCH, 32], F32)
    a1_sb = sb("a1", [128, NCH, 32], F32)
    cos_sb = sb("cos", [128, NCH, 8], F32)
    sin_sb = sb("sin", [128, NCH, 8], F32)
    gT_sb = sb("gT", [128, 3, 8], F32)
    gqk_sb = sb("gqk", [128, 768], F32)
    gkc_sb = sb("gkc", [128, 128], F32)
    convw_sb = sb("convw", [128, 4, 3], F32)
    peT_sb = sb("peT", [64, 2, 32], BF)
    w2_sb = sb("w2", [128, 2, 2, 64], BF)
    pbias_sb = sb("pbias", [128, 2, 2], F32)
    stat = sb("stat", [128, NCH, 4], F32)
    fence_t = sb("fence_t", [128, 8], F32)
    cb = P.buf("consts", arena=False)
    statb = P.bufs(NCH, "stat", arena=False)

    AR_F32 = 29696
    arena = sb("arena", [128, AR_F32], F32)

    def AV(off_kb, shape, dt, np_=128):
        o = int(round(off_kb * 256))
        n = 1
        for s_ in shape[1:]:
            n *= s_
        if dt == BF:
            assert n % 2 == 0
            n32 = n // 2
        else:
            n32 = n
        assert o + n32 <= AR_F32, (off_kb, shape)
        v = arena[0:np_, o:o + n32]
        if dt == BF:
            v = v.bitcast(BF)
        if len(shape) == 3:
            v = v.rearrange("p (a b) -> p a b", a=shape[1])
        elif len(shape) == 4:
            v = v.rearrange("p (a b c) -> p a b c", a=shape[1], b=shape[2])
        return v

    ps = es.enter_context(nc.psum_tensor("ps", [128, 4096], F32))
    pb = P.bufs(8, "psum", arena=False)
    bank_ctr = {"all": 0, "hi": 0, "pair": 0}

    def bank(pool="all"):
        if pool == "all":
            i = bank_ctr["all"] % 8
            bank_ctr["all"] += 1
        else:
            i = 4 + bank_ctr["hi"] % 4
            bank_ctr["hi"] += 1
        return i

    def pair():
        i = (bank_ctr["pair"] % 2) * 2
        bank_ctr["pair"] += 1
        return i

    def PS(i, n=512):
        return ps[:, i * 512:i * 512 + n]

    def PSB(i):
        return ps[:, i * 512:(i + 1) * 512].bitcast(BF)

    def mm(out, lhsT, rhs, start, stop, r, w):
        P.add("pe", lambda e: e.matmul(out, lhsT=lhsT, rhs=rhs, start=start, stop=stop, skip_group_check=True), r=r, w=w)

    def tr(out, in_, ident, r, w):
        P.add("pe", lambda e: e.transpose(out=out, in_=in_, identity=ident), r=r, w=w)

    def act(out, in_, func, r, w, scale=None, bias=None, accum=None):
        kw = {}
        if scale is not None:
            kw["scale"] = scale
        if bias is not None:
            kw["bias"] = bias
        if accum is not None:
            kw["accum_out"] = accum
        P.add("act", lambda e: e.activation(out=out, in_=in_, func=func, **kw), r=r, w=w)

    def tt(out, in0, in1, op, r, w, eng="dve"):
        P.add(eng, lambda e: e.tensor_tensor(out=out, in0=in0, in1=in1, op=op), r=r, w=w)

    def ts(out, in0, s1, op0, r, w, s2=None, op1=None, eng="dve"):
        if op1 is None:
            P.add(eng, lambda e: e.tensor_scalar(out=out, in0=in0, scalar1=s1, scalar2=None, op0=op0), r=r, w=w)
        else:
            P.add(eng, lambda e: e.tensor_scalar(out=out, in0=in0, scalar1=s1, scalar2=s2, op0=op0, op1=op1), r=r, w=w)

    def stt(out, in0, scalar, in1, op0, op1, r, w, eng="dve"):
        P.add(eng, lambda e: e.scalar_tensor_tensor(out=out, in0=in0, scalar=scalar, in1=in1, op0=op0, op1=op1), r=r, w=w)

    def cp(out, in_, r, w, eng="dve"):
        P.add(eng, lambda e: e.tensor_copy(out=out, in_=in_), r=r, w=w)

    def recip(out, in_, r, w):
        P.add("dve", lambda e: e.reciprocal(out=out, in_=in_), r=r, w=w)

    def red(out, in_, r, w):
        P.add("dve", lambda e: e.tensor_reduce(out=out, in_=in_, axis=AX.X, op=ALU.add), r=r, w=w)

    def memset(ap, val, r, w, eng="dve"):
        P.add(eng, lambda e: e.memset(ap, val), r=r, w=w)

    def dma(q, out, in_, r, w):
        P.dma(q, lambda e: e.dma_start(out=out, in_=in_), r=r, w=w)

    def fence():
        op = P.add("dve", lambda e: e.memset(fence_t[:, 0:1], 0.0), r=(), w=list(P.allbufs))
        P.last_fence = op.idx
        P.allbufs = [b for b in P.allbufs if b.name in ("QM", "KE", "VT", "GT", "KVC", "KCT", "VCA", "AT", "CT") or b.name.startswith("ffn_") or b.name.startswith("mixhT")]

    def bc(ap, shape):
        return ap.to_broadcast(list(shape))

    dma("pool", ident_bf[:], cd["c_ident"], [], [cb])
    dma("sp", ident_f[:], cd["c_ident"], [], [cb])
    dma("pool", tric[:], cd["c_tric"], [], [cb])
    dma("pool", triw[:], cd["c_triw"], [], [cb])
    dma("pool", tric01[:], cd["c_tric01"], [], [cb])
    dma("pool", triw01[:], cd["c_triw01"], [], [cb])
    dma("pool", cmask[:], cd["c_cmask"], [], [cb])
    dma("sp", m1_sb[:], cd["c_m1"], [], [cb])
    dma("sp", a1_sb[:], cd["c_a1"], [], [cb])
    dma("sp", cos_sb[:], cd["c_cos"], [], [cb])
    dma("sp", sin_sb[:], cd["c_sin"], [], [cb])
    dma("sp", gT_sb[:], gT_d, [], [cb])
    dma("sp", gqk_sb[:], gqk_d, [], [cb])
    dma("sp", gkc_sb[:], gkc_d, [], [cb])
    dma("sp", convw_sb[:], convw_d, [], [cb])
    dma("pool", peT_sb[:], peT_d, [], [cb])
    for slot in range(2):
        dma("pool", w2_sb[:, slot, :, :], w2_d[slot].rearrange("(hc p) d -> p hc d", p=128), [], [cb])

    def norm_chunk(c, gi, xhat_v, xhat_b, junk_v, junk_b, dst, dst_b):
        xs = x_sb[:, c, :]
        st = stat[:, c, :]
        act(junk_v, xs, AF.Square, [xb[c]], [junk_b, statb[c]], accum=st[:, 0:1])
        act(st[:, 1:2], st[:, 0:1], AF.Sqrt, [statb[c]], [statb[c]], scale=1.0 / D, bias=EPS)
        recip(st[:, 2:3], st[:, 1:2], [statb[c]], [statb[c]])
        act(xhat_v, xs, AF.Copy, [xb[c], statb[c]], [xhat_b], scale=st[:, 2:3])
        norm_b(gi, xhat_v, xhat_b, dst, dst_b)

    def norm_a(c, xhat_v, xhat_b, junk_v, junk_b):
        xs = x_sb[:, c, :]
        st = stat[:, c, :]
        act(junk_v, xs, AF.Square, [xb[c]], [junk_b, statb[c]], accum=st[:, 0:1])
        act(st[:, 1:2], st[:, 0:1], AF.Sqrt, [statb[c]], [statb[c]], scale=1.0 / D, bias=EPS)
        recip(st[:, 2:3], st[:, 1:2], [statb[c]], [statb[c]])
        act(xhat_v, xs, AF.Copy, [xb[c], statb[c]], [xhat_b], scale=st[:, 2:3])

    def norm_b(gi, xhat_v, xhat_b, dst, dst_b):
        bk = bank("all")
        pv = PSB(bk).rearrange("p (a b) -> p a b", a=8)
        for kc in range(8):
            tr(pv[:, kc, :], xhat_v[:, kc * 128:(kc + 1) * 128], ident_bf[:], [xhat_b, cb], [pb[bk]])
        tt(dst, pv, bc(gT_sb[:, gi, :].unsqueeze(2), [128, 8, 128]), ALU.mult, [pb[bk], cb], [dst_b])

    ffn_state = {}

    def ffn(fi, gi, fence_after=True):
        hT = AV(0, [128, 8, S], BF)
        if "b" not in ffn_state:
            ffn_state["b"] = dict(hTb=P.bufs(4, "ffn_hT"), w=[(P.buf("ffn_wg%d" % i), P.buf("ffn_wu%d" % i), P.buf("ffn_wd%d" % i)) for i in range(2)],
                                  act=[P.bufs(4, "ffn_act0_"), P.bufs(4, "ffn_act1_")], junk=P.buf("ffn_junk"),
                                  xh=[P.buf("ffn_xh%d" % i) for i in range(4)], sg=[P.buf("ffn_sg0"), P.buf("ffn_sg1")])
        fb = ffn_state["b"]
        hTb = fb["hTb"]
        wts = []
        for s_ in range(2):
            o = 32 + 18 * s_
            wts.append((AV(o, [128, 8, 384], BF), AV(o + 6, [128, 8, 384], BF), AV(o + 12, [128, 3, D], BF)) + fb["w"][s_])
        acts = [(AV(68, [128, 3, S], BF), fb["act"][0]), (AV(80, [128, 3, S], BF), fb["act"][1])]
        junk = AV(92, [128, D], BF)
        junk_b = fb["junk"]
        xh = [(AV(o, [128, D], BF), fb["xh"][i]) for i, o in enumerate((94, 96, 102, 104))]
        sg = [(AV(98, [128, 512], F32), fb["sg"][0]), (AV(100, [128, 512], F32), fb["sg"][1])]
        groups = [(f0, min(3, 22 - f0)) for f0 in range(0, 22, 3)]

        def load_w(gi_):
            f0, nf = groups[gi_]
            wgt, wut, wdt, bg, bu, bd = wts[gi_ % 2]
            dma("pool", wgt[:, :, 0:nf * 128], wg_d[fi][:, f0 * 128:(f0 + nf) * 128].rearrange("(kc p) f -> p kc f", p=128), [], [bg])
            dma("pool", wut[:, :, 0:nf * 128], wu_d[fi][:, f0 * 128:(f0 + nf) * 128].rearrange("(kc p) f -> p kc f", p=128), [], [bu])
            dma("pool", wdt[:, 0:nf, :], wd_d[fi][f0 * 128:(f0 + nf) * 128, :].rearrange("(j p) d -> p j d", p=128), [], [bd])

        load_w(0)

        def nA(T):
            for c in range(4 * T, 4 * T + 4):
                xv, xbuf = xh[c % 4]
                norm_a(c, xv, xbuf, junk, junk_b)

        def nB(T):
            for c in range(4 * T, 4 * T + 4):
                xv, xbuf = xh[c % 4]
                norm_b(gi, xv, xbuf, hT[:, :, c * 128:(c + 1) * 128], hTb[c // 4])

        nA(0)
        nB(0)
        nA(1)
        sgi = 0
        for gidx, (f0, nf) in enumerate(groups):
            if gidx + 1 < len(groups):
                load_w(gidx + 1)
            wgt, wut, wdt, bg, bu, bd = wts[gidx % 2]
            actv, actb = acts[gidx % 2]
            for T in range(4):
                if gidx == 0 and T >= 1:
                    nB(T)
                    if T + 1 < 4:
                        nA(T + 1)
                for j in range(nf):
                    ba = bank("all")
                    bu_ = bank("all")
                    for kc in range(8):
                        mm(PS(ba), wgt[:, kc, j * 128:(j + 1) * 128], hT[:, kc, T * 512:(T + 1) * 512], kc == 0, kc == 7,
                           [bg, hTb[T]], [pb[ba]])
                    for kc in range(8):
                        mm(PS(bu_), wut[:, kc, j * 128:(j + 1) * 128], hT[:, kc, T * 512:(T + 1) * 512], kc == 0, kc == 7,
                           [bu, hTb[T]], [pb[bu_]])
                    sgv, sgb = sg[sgi % 2]
                    sgi += 1
                    act(sgv, PS(ba), AF.Silu, [pb[ba]], [sgb])
                    tt(actv[:, j, T * 512:(T + 1) * 512], sgv, PS(bu_), ALU.mult, [sgb, pb[bu_]], [actb[T]])
            for c in range(NCH):
                for half in range(2):
                    bo = bank("all")
                    for j in range(nf):
                        mm(PS(bo), actv[:, j, c * 128:(c + 1) * 128], wdt[:, j, half * 512:(half + 1) * 512], j == 0, j == nf - 1,
                           [actb[c // 4], bd], [pb[bo]])
                    xs = x_sb[:, c, half * 512:(half + 1) * 512]
                    stt(xs, PS(bo), 0.5, xs, ALU.mult, ALU.add, [pb[bo], xb[c]], [xb[c]])
        if fence_after:
            fence()

    QM = AV(0, [96, 8, S], BF, 96)
    KE = AV(32, [96, 4, S], BF, 96)
    VT = AV(48, [128, NCH, 4, 65], BF)
    GT = AV(56.5, [128, NCH, 24], F32)
    KVC = AV(59, [128, 2, S], BF)
    KCT = AV(58, [64, 2, 128], BF, 64)
    VCA = AV(58.5, [127, 2, 98], BF, 127)
    AT = AV(88, [128, 4, S], BF)
    CT = AV(16, [128, 4, S], BF)

    def dbg_dump(name, view, r, q="sp"):
        if dbg and name in dbg_d:
            dma(q, dbg_d[name], view, r, [])

    def phase_c(bufs):
        qmb, keb, vtb, gtb, kvcb = bufs
        hTt = [(AV(67, [128, 8, 512], BF), P.buf())] * 2
        wtok = AV(75, [128, 8, 1048], BF)
        wtok_b = P.buf()
        wfeat = AV(92, [128, 8, 256], BF)
        wfeat_b = P.buf()
        sqs = [(AV(96, [128, 768], F32), P.buf()), (AV(104, [128, 768], F32), P.buf())]
        junk = AV(96, [128, D], BF)
        junk_b = sqs[0][1]
        xh = [(AV(99, [128, D], BF), P.buf()), (AV(108.5, [128, D], BF), P.buf())]
        qbfs = [(AV(101, [128, 768], BF), P.buf()), (AV(107, [128, 768], BF), P.buf())]
        rope_t = AV(102.5, [128, 4, 96], F32)
        segs = [(0, 512, 0), (768, 896, 512), (1024, 1152, 640), (896, 1024, 768), (1152, 1280, 896), (1280, 1304, 1024)]
        dma("pool", wfeat[:], win_d[:, 512:768].rearrange("(kc p) f -> p kc f", p=128), [], [wfeat_b])
        wq_b, wkv_b, wgt_b = P.buf(), P.buf(), P.buf()
        for (a_, b_, o) in segs:
            bb_ = wq_b if o == 0 else (wgt_b if o == 1024 else wkv_b)
            dma("pool", wtok[:, :, o:o + (b_ - a_)], win_d[:, a_:b_].rearrange("(kc p) f -> p kc f", p=128), [], [bb_])
        for g in range(2):
            dma("pool", KE[64:96, g, :], cd["c_efull"], [], [keb])
        memset(VT[:, :, :, 64:65], 1.0, [], [vtb])

        def norm_c(c):
            T, cc = divmod(c, 4)
            hv, hb = hTt[T % 2]
            xv, xbuf = xh[c % 2]
            norm_chunk(c, 1, xv, xbuf, junk, junk_b, hv[:, :, cc * 128:(cc + 1) * 128], hb)

        def feat(T):
            hv, hb = hTt[T % 2]
            for slot in range(2):
                bk = bank("hi")
                for kc in range(8):
                    mm(PS(bk), wfeat[:, kc, slot * 128:(slot + 1) * 128], hv[:, kc, :], kc == 0, kc == 7, [wfeat_b, hb], [pb[bk]])
                act(KVC[:, slot, T * 512:(T + 1) * 512], PS(bk), AF.Copy, [pb[bk]], [kvcb])

        def s1(c):
            T, cc = divmod(c, 4)
            hv, hb = hTt[T % 2]
            sq, sq_b = sqs[c % 2]
            qf, qf_b = sq, sq_b
            qbf, qbf_b = qbfs[c % 2]
            pr = pair()
            pq = ps[:, pr * 512:pr * 512 + 1024]
            for kc in range(8):
                mm(pq[:, 0:512], hv[:, kc, cc * 128:(cc + 1) * 128], wtok[:, kc, 0:512], kc == 0, kc == 7, [hb, wq_b], [pb[pr]])
            for kc in range(8):
                mm(pq[:, 512:1024], hv[:, kc, cc * 128:(cc + 1) * 128], wtok[:, kc, 512:1024], kc == 0, kc == 7, [hb, wkv_b], [pb[pr + 1]])
            bg_ = bank("hi")
            for kc in range(8):
                mm(PS(bg_, 24), hv[:, kc, cc * 128:(cc + 1) * 128], wtok[:, kc, 1024:1048], kc == 0, kc == 7, [hb, wgt_b], [pb[bg_]])
            prb = [pb[pr], pb[pr + 1]]
            act(sq, pq[:, 0:768], AF.Square, prb, [sq_b])
            red(S12[:, 0, :], sq.rearrange("p (h d) -> p h d", d=64), [sq_b], [s12_b])
            act(S12[:, 1, :], S12[:, 0, :], AF.Sqrt, [s12_b], [s12_b], scale=1.0 / 64, bias=EPS)
            recip(S12[:, 2, :], S12[:, 1, :], [s12_b], [s12_b])
            qf3 = qf.rearrange("p (h d) -> p h d", d=64)
            tt(qf3, pq[:, 0:768].rearrange("p (h d) -> p h d", d=64), bc(S12[:, 2, :].unsqueeze(2), [128, 12, 64]), ALU.mult,
               prb + [s12_b], [qf_b])
            act(VT[:, c, :, 0:64], pq[:, 768:1024].rearrange("p (a d) -> p a d", d=64), AF.Copy, prb, [vtb])
            act(GT[:, c, :], PS(bg_, 24), AF.Sigmoid, [pb[bg_]], [gtb])
            tt(qf, qf, gqk_sb[:], ALU.mult, [qf_b, cb], [qf_b])
            cosb = bc(cos_sb[:, c, :].unsqueeze(1), [128, 12, 8])
            sinb = bc(sin_sb[:, c, :].unsqueeze(1), [128, 12, 8])
            x1 = qf3[:, :, 0:8]
            x2 = qf3[:, :, 8:16]
            rt = rope_t.rearrange("p a (h d) -> p a h d", d=8)
            tt(rt[:, 0], x1, cosb, ALU.mult, [qf_b, cb], [rope_b])
            tt(rt[:, 1], x2, sinb, ALU.mult, [qf_b, cb], [rope_b])
            tt(rt[:, 2], x2, cosb, ALU.mult, [qf_b, cb], [rope_b])
            tt(rt[:, 3], x1, sinb, ALU.mult, [qf_b, cb], [rope_b])
            tt(x1, rt[:, 0], rt[:, 1], ALU.subtract, [rope_b], [qf_b])
            tt(x2, rt[:, 2], rt[:, 3], ALU.add, [rope_b], [qf_b])
            cp(qbf, qf, [qf_b], [qbf_b])

        def s2(c):
            qbf, qbf_b = qbfs[c % 2]
            bq = bank("hi")
            pqv = PSB(bq).rearrange("p (a b) -> p a b", a=8)
            for h in range(8):
                tr(pqv[0:64, h, :], qbf[:, h * 64:(h + 1) * 64], ident_bf[:], [qbf_b, cb], [pb[bq]])
            act(QM[0:64, :, c * 128:(c + 1) * 128], pqv[0:64], AF.Copy, [pb[bq]], [qmb])
            bk_ = bank("hi")
            pkv = PSB(bk_).rearrange("p (a b) -> p a b", a=8)
            for i in range(4):
                tr(pkv[0:64, i, :], qbf[:, 512 + i * 64:512 + (i + 1) * 64], ident_bf[:], [qbf_b, cb], [pb[bk_]])
            cp(KE[0:64, :, c * 128:(c + 1) * 128], pkv[0:64, 0:4, :], [pb[bk_]], [keb])

        for c in range(4):
            norm_c(c)
        for step in range(NCH + 1):
            if step < NCH:
                T, cc = divmod(step, 4)
                if cc == 0:
                    feat(T)
                s1(step)
                if cc == 3 and step + 1 < NCH:
                    for c2 in range(step + 1, step + 5):
                        norm_c(c2)
            if 0 <= step - 1 < NCH:
                s2(step - 1)
        fence()

    S12 = sb("s12", [128, 3, 12], F32)
    s12_b = P.buf("s12", arena=False)
    rope_b = P.buf("rope")

    def load_w1(slot, w1t, w1b):
        src = w1_d[slot].rearrange("(t d) h -> d t h", d=64)
        dma("pool", w1t[0:64], src, [], [w1b])
        dma("pool", w1t[64:128], src, [], [w1b])

    def prologue_pbias():
        w1t = AV(67, [128, 32, 256], BF)
        w1b = P.buf()
        for slot in range(2):
            load_w1(slot, w1t, w1b)
            for hc in range(2):
                bk = bank("all")
                for t in range(32):
                    mm(ps[:, bk * 512:bk * 512 + 1], w1t[0:64, t, hc * 128:(hc + 1) * 128], peT_sb[0:64, slot, t:t + 1], t == 0, t == 31,
                       [w1b, cb], [pb[bk]])
                cp(pbias_sb[:, slot, hc:hc + 1], ps[:, bk * 512:bk * 512 + 1], [pb[bk]], [cb])
        fence()

    def phase_d(bufs):
        qmb, keb, vtb, gtb, kvcb, kctb, vcab = bufs
        w1g = [AV(67, [128, 32, 256], BF), AV(88, [128, 32, 256], BF)]
        w1bs = [[P.buf(), P.buf()], [P.buf(), P.buf()]]
        hact = AV(83, [128, 2, 254], BF)
        hact_b = P.buf()
        ksq = AV(84, [128, 128], F32)
        ksq_b = P.buf()
        kcn = AV(84.5, [128, 128], F32)
        kcn_b = P.buf()
        kcb16 = AV(85, [128, 128], BF)
        kcb16_b = P.buf()
        dma("pool", VCA[:, :, 64:97], cd["c_vca"], [], [vcab])
        for th in range(2):
            memset(w1g[0][64:128, th * 16:(th + 1) * 16, :], 0.0, [], [w1bs[0][th]])
            memset(w1g[1][0:64, th * 16:(th + 1) * 16, :], 0.0, [], [w1bs[1][th]])
        for slot in range(2):
            src = w1_d[slot].rearrange("(t d) h -> d t h", d=64)
            for th in range(2):
                tsl = slice(th * 16, (th + 1) * 16)
                dma("pool", w1g[0][0:64, tsl, :], src[:, tsl, :], [], [w1bs[0][th]])
                dma("sp", w1g[1][64:128, tsl, :], w1g[0][0:64, tsl, :], [w1bs[0][th]], [w1bs[1][th]])
            bks = {(hc, g): bank("all") for hc in range(2) for g in range(2)}
            for th in range(2):
                for hc in range(2):
                    for g in range(2):
                        bk = bks[(hc, g)]
                        for t in range(th * 16, (th + 1) * 16):
                            mm(ps[:, bk * 512:bk * 512 + 127], w1g[g][:, t, hc * 128:(hc + 1) * 128],
                               KVC[:, slot, t:t + 16 * 126 + 1:16], t == 0, t == 31, [w1bs[g][th], kvcb], [pb[bk]])
            for hc in range(2):
                for g in range(2):
                    bk = bks[(hc, g)]
                    act(hact[:, hc, g * 127:(g + 1) * 127], ps[:, bk * 512:bk * 512 + 127], AF.Silu, [pb[bk], cb], [hact_b],
                        bias=pbias_sb[:, slot, hc:hc + 1])
            bk = bank("all")
            for g in range(2):
                for hc in range(2):
                    mm(ps[0:127, bk * 512 + g * 64:bk * 512 + (g + 1) * 64], hact[:, hc, g * 127:(g + 1) * 127], w2_sb[:, slot, hc, :],
                       hc == 0, hc == 1, [hact_b, cb], [pb[bk]])
            pk = ps[0:127, bk * 512:bk * 512 + 128]
            if slot == 0:
                act(ksq[0:127], pk, AF.Square, [pb[bk]], [ksq_b])
                red(S12[0:127, 0, 0:2], ksq[0:127].rearrange("p (g d) -> p g d", d=64), [ksq_b], [s12_b])
                act(S12[0:127, 1, 0:2], S12[0:127, 0, 0:2], AF.Sqrt, [s12_b], [s12_b], scale=1.0 / 64, bias=EPS)
                recip(S12[0:127, 2, 0:2], S12[0:127, 1, 0:2], [s12_b], [s12_b])
                tt(kcn[0:127].rearrange("p (g d) -> p g d", d=64), pk.rearrange("p (g d) -> p g d", d=64),
                   bc(S12[0:127, 2, 0:2].unsqueeze(2), [127, 2, 64]), ALU.mult, [pb[bk], s12_b], [kcn_b])
                tt(kcb16[0:127], kcn[0:127], gkc_sb[0:127], ALU.mult, [kcn_b, cb], [kcb16_b])
                bt = bank("all")
                ptv = PSB(bt).rearrange("p (a b) -> p a b", a=8)
                for g in range(2):
                    tr(ptv[0:64, g, 0:127], kcb16[0:127, g * 64:(g + 1) * 64], ident_bf[0:127, 0:127], [kcb16_b, cb], [pb[bt]])
                act(KCT[0:64, :, 0:127], ptv[0:64, 0:2, 0:127], AF.Copy, [pb[bt]], [kctb])
            else:
                act(VCA[:, :, 0:64], pk.rearrange("p (g d) -> p g d", d=64), AF.Copy, [pb[bk]], [vcab])
        fence()

    def phase_ef(bufs):
        qmb, keb, vtb, gtb, kvcb, kctb, vcab, atb = bufs
        qmask_b = P.bufs(4, "qmask")
        ptc = AV(59, [127, 4, 512], BF, 127)
        ptc_b = P.bufs(4, "ptc")
        NPT = 6
        LOOK = 5
        pts = [(AV(o, [128, 512], BF), P.buf()) for o in (63, 64, 65, 66, 114, 115)]
        otsb = [(AV(67 + 2 * i, [65, 512], F32, 65), P.buf()) for i in range(2)]
        rank = AV(71, [128, 32, 32], F32)
        rank_b = P.buf()
        sm = AV(75, [128, 512], F32)
        rs4 = sm[:, 0:4]
        ri4 = sm[:, 8:12]
        cfc = sm[:, 16:20]
        impw = sm[:, 32:160]
        imp = sm[:, 288:320]
        cnt = sm[:, 352:384]
        r4 = sm[:, 416:420]
        cf4 = sm[:, 420:424]
        sm_b = P.buf()
        sm2_b = P.buf()
        sm3_b = P.buf()
        cnt2 = AV(77.5, [128, 32], F32)
        cnt_b = P.buf()
        negm = AV(77, [128, 32], BF)
        negm_b = P.buf()
        tmpo = AV(78.5, [128, 4, 64], F32)
        tmpo_b = P.buf()
        atoks = [(AV(80, [128, 4, 512], F32), P.bufs(4, "atok0")), (AV(106, [128, 4, 512], F32), P.bufs(4, "atok1"))]
        abfs = [(AV(104 + i, [128, 512], BF), P.buf()) for i in range(2)]
        ctr = {"pt": 0, "ot": 0, "s": 0, "o": 0, "t": 0}

        def sbank():
            ctr["s"] += 1
            return ctr["s"] % 4

        def obank():
            ctr["o"] += 1
            return 4 + ctr["o"] % 2

        def tbank():
            ctr["t"] += 1
            return 6 + ctr["t"] % 2

        def e_tasks(T):
            atok, atok_b = atoks[T % 2]
            tasks = []
            for g in range(2):
                def t_exp(g=g):
                    for hh in range(4):
                        h = g * 4 + hh
                        bs = sbank()
                        mm(ps[0:127, bs * 512:(bs + 1) * 512], KCT[0:64, g, 0:127], QM[0:64, h, T * 512:(T + 1) * 512], True, True, [kctb, qmb], [pb[bs]])
                        act(ptc[:, hh, :], ps[0:127, bs * 512:(bs + 1) * 512], AF.Exp, [pb[bs]], [ptc_b[hh]], scale=0.125)
                        tt(ptc[:, hh, :], ptc[:, hh, :], cmask[:, T, :], ALU.mult, [ptc_b[hh], cb], [ptc_b[hh]], eng="pool")
                tasks.append(t_exp)
                for cc in range(4):
                    def t_cc_a(g=g, cc=cc):
                        c = 4 * T + cc
                        bp = tbank()
                        for hh in range(4):
                            o = bp * 512 + hh * 97
                            mm(ps[:, o:o + 97], ptc[:, hh, cc * 128:(cc + 1) * 128], VCA[:, g, 0:97], True, True, [ptc_b[hh], vcab], [pb[bp]])
                        pc = ps[:, bp * 512:bp * 512 + 388].rearrange("p (h x) -> p h x", x=97)
                        prb = [pb[bp]]
                        ts(rs4.unsqueeze(2), pc[:, :, 64:65], 1e-30, ALU.max, prb, [sm_b])
                        recip(ri4, rs4, [sm_b], [sm_b])
                        tt(cfc, ri4, GT[:, c, g * 4:(g + 1) * 4], ALU.mult, [sm_b, gtb], [sm_b])
                        tt(atok[:, cc, g * 256:(g + 1) * 256].rearrange("p (h d) -> p h d", d=64), pc[:, :, 0:64],
                           bc(cfc.unsqueeze(2), [128, 4, 64]), ALU.mult, prb + [sm_b], [atok_b[cc]])
                        tt(impw.rearrange("p (h j) -> p h j", j=32), pc[:, :, 65:97], bc(ri4.unsqueeze(2), [128, 4, 32]), ALU.mult,
                           prb + [sm_b], [sm2_b])
                        red(imp, impw.rearrange("p (h j) -> p j h", j=32), [sm2_b], [sm2_b])
                        tt(imp, imp, m1_sb[:, c, :], ALU.mult, [sm2_b, cb], [sm2_b])
                        tt(imp, imp, a1_sb[:, c, :], ALU.add, [sm2_b, cb], [sm2_b])
                        tt(rank, bc(imp.unsqueeze(1), [128, 32, 32]), bc(imp.unsqueeze(2), [128, 32, 32]), ALU.is_gt, [sm2_b], [rank_b])
                        red(cnt2, rank, [rank_b], [cnt_b])
                        ts(negm, cnt2, 16.0, ALU.is_ge, [cnt_b], [negm_b], s2=NEG, op1=ALU.mult)

                    def t_cc_b(g=g, cc=cc):
                        c = 4 * T + cc
                        bt = tbank()
                        ptv = PSB(bt).rearrange("p (a b) -> p a b", a=8)
                        tr(ptv[0:32, 0, :], negm[:, :], ident_bf[:], [negm_b, cb], [pb[bt]])
                        cp(QM[64:96, g * 4:(g + 1) * 4, c * 128:(c + 1) * 128], bc(ptv[0:32, 0:1, :], [32, 4, 128]), [pb[bt]], [qmask_b[T]])
                    tasks.append(t_cc_a)
                    tasks.append(t_cc_b)
            return tasks

        def f_items(T):
            items = []
            for br in (1, 0):
                for h in range(8):
                    g = h // 4
                    lst = []
                    if br == 0:
                        for kc in range(4 * T + 4):
                            if kc < 4 * T:
                                lst.append((kc, 0, 512, None, None))
                            else:
                                i = kc - 4 * T
                                lst.append((kc, 128 * i, 512, 128 * i, tric))
                    else:
                        for kc in range(max(0, 4 * T - 4), 4 * T + 4):
                            if kc < 4 * T:
                                i = kc - (4 * T - 4)
                                lst.append((kc, 0, 128 * (i + 1), 128 * i, triw))
                            else:
                                i = kc - 4 * T
                                lst.append((kc, 128 * i, 512, 128 * i, tric))
                    for n_i, it in enumerate(lst):
                        items.append((br, h, g, n_i == 0, n_i == len(lst) - 1) + it)
            return items

        def run_f(T, extra_tasks):
            atok, atok_b = atoks[T % 2]
            items = f_items(T)
            n = len(items)
            state = {}
            posts = []
            every = max(1, n // (len(extra_tasks) + 1)) if extra_tasks else n + 1
            extra = list(extra_tasks)
            cur_o = {}
            for idx in range(n + LOOK):
                if idx < n:
                    br, h, g, first, last, kc, c0, c1, m0, mk = items[idx]
                    K = 96 if br == 0 else 64
                    kidx = g if br == 0 else 2 + g
                    bs = sbank()
                    rd = [keb, qmb] + ([qmask_b[T]] if br == 0 else [])
                    mm(ps[:, bs * 512 + c0:bs * 512 + c1], KE[0:K, kidx, kc * 128:(kc + 1) * 128], QM[0:K, h, T * 512 + c0:T * 512 + c1],
                       True, True, rd, [pb[bs]])
                    ptv_, ptb_ = pts[ctr["pt"] % NPT]
                    ctr["pt"] += 1
                    act(ptv_[:, c0:c1], ps[:, bs * 512 + c0:bs * 512 + c1], AF.Exp, [pb[bs]], [ptb_], scale=0.125)
                    if mk is not None:
                        mk01 = tric01 if mk is tric else triw01
                        tt(ptv_[:, m0:m0 + 128], ptv_[:, m0:m0 + 128], mk01[:], ALU.mult, [ptb_, cb], [ptb_], eng="pool")
                    state[idx] = (ptv_, ptb_)
                j = idx - LOOK
                if j >= 0:
                    br, h, g, first, last, kc, c0, c1, m0, mk = items[j]
                    ptv_, ptb_ = state.pop(j)
                    if first:
                        cur_o[(br, h)] = obank()
                    bo = cur_o[(br, h)]
                    vidx = br * 2 + g
                    mm(ps[0:65, bo * 512 + c0:bo * 512 + c1], VT[:, kc, vidx, 0:65], ptv_[:, c0:c1], first, last, [vtb, ptb_], [pb[bo]])
                    if last:
                        ov, ob = otsb[ctr["ot"] % 2]
                        ctr["ot"] += 1
                        cp(ov, ps[0:65, bo * 512:(bo + 1) * 512], [pb[bo]], [ob])

                        def post(br=br, h=h, ov=ov, ob=ob):
                            bt = tbank()
                            tov = ps[:, bt * 512:bt * 512 + 260].rearrange("p (a x) -> p a x", x=65)
                            for cc in range(4):
                                tr(tov[:, cc, :], ov[0:65, cc * 128:(cc + 1) * 128], ident_f[0:65, 0:65], [ob, cb], [pb[bt]])
                            recip(r4.unsqueeze(2), tov[:, :, 64:65], [pb[bt]], [sm3_b])
                            tt(cf4, r4, GT[:, 4 * T:4 * T + 4, 8 + br * 8 + h], ALU.mult, [sm3_b, gtb], [sm3_b])
                            tt(tmpo, tov[:, :, 0:64], bc(cf4.unsqueeze(2), [128, 4, 64]), ALU.mult, [pb[bt], sm3_b], [tmpo_b])
                            av = atok[:, :, h * 64:(h + 1) * 64]
                            tt(av, av, tmpo, ALU.add, [tmpo_b] + atok_b, atok_b)
                        posts.append((idx + 4, post))
                while posts and posts[0][0] <= idx:
                    posts.pop(0)[1]()
                if extra and idx % every == every - 1:
                    extra.pop(0)()
            for _, p_ in posts:
                p_()
            for t_ in extra:
                t_()
            for cc in range(4):
                c = 4 * T + cc
                abv, abb = abfs[cc % 2]
                cp(abv, atok[:, cc, :], atok_b, [abb])
                bt = tbank()
                ptv = PSB(bt).rearrange("p (a b) -> p a b", a=8)
                for j_ in range(4):
                    tr(ptv[:, j_, :], abv[:, j_ * 128:(j_ + 1) * 128], ident_bf[:], [abb, cb], [pb[bt]])
                act(AT[:, :, c * 128:(c + 1) * 128], ptv[:, 0:4, :], AF.Copy, [pb[bt]], [atb])

        for t_ in e_tasks(0):
            t_()
        for T in range(4):
            run_f(T, e_tasks(T + 1) if T < 3 else [])
        fence()

    def phase_h(bufs):
        atb, ctb, hTt = bufs
        wcv = AV(32, [128, 8, 1536], BF)
        wcv_b = P.buf()
        junk = AV(81, [128, D], BF)
        junk_b = P.buf()
        xh = [(AV(83, [128, D], BF), P.buf()), (AV(106, [128, D], BF), P.buf())]
        usb = AV(85, [128, 512], F32)
        usb_b = P.buf()
        cu = AV(56, [128, 4, 516], F32)
        cu_b = P.bufs(4, "cu")
        acc = AV(104, [128, 512], F32)
        acc_b = P.buf()
        tap = AV(108, [128, 512], F32)
        tap_b = P.buf()
        wcv_fb = P.bufs(4, "wcvf")
        for fc in range(4):
            for sel in range(3):
                o = sel * 512 + fc * 128
                dma("pool", wcv[:, :, o:o + 128], win_d[:, 1304 + o:1304 + o + 128].rearrange("(kc p) f -> p kc f", p=128), [], [wcv_fb[fc]])
        for fc in range(4):
            memset(cu[:, fc, 0:2], 0.0, [], [cu_b[fc]])
        def normT(T):
            hv, hb = hTt[T]
            for cc in range(4):
                c = 4 * T + cc
                xv, xbuf = xh[c % 2]
                norm_chunk(c, 1, xv, xbuf, junk, junk_b, hv[:, :, cc * 128:(cc + 1) * 128], hb)

        normT(0)
        for T in range(4):
            hv, hb = hTt[T]
            for fc in range(4):
                if fc == 1 and T + 1 < 4:
                    normT(T + 1)
                bks = []
                for sel in range(3):
                    bk = bank("all")
                    bks.append(bk)
                    for kc in range(8):
                        mm(PS(bk), wcv[:, kc, sel * 512 + fc * 128:sel * 512 + (fc + 1) * 128], hv[:, kc, :], kc == 0, kc == 7,
                           [wcv_fb[fc], hb], [pb[bk]])
                bB, bC, bU = bks
                act(usb, PS(bU), AF.Copy, [pb[bU]], [usb_b])
                tt(cu[:, fc, 2:514], PS(bC), usb, ALU.mult, [pb[bC], usb_b], [cu_b[fc]])
                ts(acc, cu[:, fc, 2:514], convw_sb[:, fc, 2:3], ALU.mult, [cu_b[fc], cb], [acc_b])
                stt(acc, cu[:, fc, 1:513], convw_sb[:, fc, 1:2], acc, ALU.mult, ALU.add, [cu_b[fc], cb, acc_b], [acc_b])
                stt(acc, cu[:, fc, 0:512], convw_sb[:, fc, 0:1], acc, ALU.mult, ALU.add, [cu_b[fc], cb, acc_b], [acc_b])
                tt(CT[:, fc, T * 512:(T + 1) * 512], acc, PS(bB), ALU.mult, [acc_b, pb[bB]], [ctb])
                cp(cu[:, fc, 0:2], cu[:, fc, 512:514], [cu_b[fc]], [cu_b[fc]])
        fence()

    def phase_i(bufs):
        atb, ctb, hTt = bufs
        wga = AV(32, [128, 8, 512], BF)
        wgc = AV(40, [128, 8, 512], BF)
        wa = AV(48, [128, 4, 512], BF)
        wc = AV(52, [128, 4, 512], BF)
        wo = AV(56, [128, 4, D], BF)
        wga_b, wgc_b, wa_b, wc_b, wo_b = P.buf(), P.buf(), P.buf(), P.buf(), P.buf()
        sgas = [(AV(84, [128, 512], F32), P.buf()), (AV(104, [128, 512], F32), P.buf())]
        sgcs = [(AV(86, [128, 512], F32), P.buf()), (AV(106, [128, 512], F32), P.buf())]
        mbs = [(AV(108, [128, 512], BF), P.buf()), (AV(109, [128, 512], BF), P.buf())]
        mTs = [(AV(110, [128, 4, 128], BF), P.buf()), (AV(111, [128, 4, 128], BF), P.buf())]
        for half in range(2):
            hs = slice(half * 512, (half + 1) * 512)
            dma("pool", wga[:], win_d[:, 2840 + half * 512:2840 + (half + 1) * 512].rearrange("(kc p) f -> p kc f", p=128), [], [wga_b])
            dma("pool", wgc[:], win_d[:, 3864 + half * 512:3864 + (half + 1) * 512].rearrange("(kc p) f -> p kc f", p=128), [], [wgc_b])
            dma("pool", wa[:], wa_d[:, hs].rearrange("(kc p) f -> p kc f", p=128), [], [wa_b])
            dma("pool", wc[:], wc_d[:, hs].rearrange("(kc p) f -> p kc f", p=128), [], [wc_b])
            dma("pool", wo[:], wo_d[half * 512:(half + 1) * 512, :].rearrange("(kc p) f -> p kc f", p=128), [], [wo_b])
            def s1(c, half=half):
                T, cc = divmod(c, 4)
                hv, hb = hTt[T]
                sga, sga_b = sgas[c % 2]
                sgc, sgc_b = sgcs[c % 2]
                mb, mb_b = mbs[c % 2]
                bga, bgc, baw, bcw = bank("all"), bank("all"), bank("all"), bank("all")
                for kc in range(8):
                    mm(PS(bga), hv[:, kc, cc * 128:(cc + 1) * 128], wga[:, kc, :], kc == 0, kc == 7, [hb, wga_b], [pb[bga]])
                for kc in range(8):
                    mm(PS(bgc), hv[:, kc, cc * 128:(cc + 1) * 128], wgc[:, kc, :], kc == 0, kc == 7, [hb, wgc_b], [pb[bgc]])
                for j in range(4):
                    mm(PS(baw), AT[:, j, c * 128:(c + 1) * 128], wa[:, j, :], j == 0, j == 3, [atb, wa_b], [pb[baw]])
                for j in range(4):
                    mm(PS(bcw), CT[:, j, c * 128:(c + 1) * 128], wc[:, j, :], j == 0, j == 3, [ctb, wc_b], [pb[bcw]])
                act(sga, PS(bga), AF.Sigmoid, [pb[bga]], [sga_b])
                act(sgc, PS(bgc), AF.Sigmoid, [pb[bgc]], [sgc_b])
                tt(sga, sga, PS(baw), ALU.mult, [sga_b, pb[baw]], [sga_b])
                tt(sgc, sgc, PS(bcw), ALU.mult, [sgc_b, pb[bcw]], [sgc_b])
                tt(mb, sga, sgc, ALU.add, [sga_b, sgc_b], [mb_b])

            def s2(c):
                mb, mb_b = mbs[c % 2]
                mT, mT_b = mTs[c % 2]
                bt = bank("all")
                ptv = PSB(bt).rearrange("p (a b) -> p a b", a=8)
                for j in range(4):
                    tr(ptv[:, j, :], mb[:, j * 128:(j + 1) * 128], ident_bf[:], [mb_b, cb], [pb[bt]])
                act(mT, ptv[:, 0:4, :], AF.Copy, [pb[bt]], [mT_b])

            def s3(c):
                mT, mT_b = mTs[c % 2]
                for nh in range(2):
                    bo = bank("all")
                    for j in range(4):
                        mm(PS(bo), mT[:, j, :], wo[:, j, nh * 512:(nh + 1) * 512], j == 0, j == 3, [mT_b, wo_b], [pb[bo]])
                    xs = x_sb[:, c, nh * 512:(nh + 1) * 512]
                    tt(xs, xs, PS(bo), ALU.add, [xb[c], pb[bo]], [xb[c]])

            for step in range(NCH + 2):
                if step < NCH:
                    s1(step)
                if 0 <= step - 2 < NCH:
                    s3(step - 2)
                if 0 <= step - 1 < NCH:
                    s2(step - 1)
        fence()

    prologue_pbias()
    qmb, keb, vtb, gtb, kvcb, kctb, vcab, atb, ctb = (P.buf("QM"), P.buf("KE"), P.buf("VT"), P.buf("GT"), P.buf("KVC"),
                                                     P.buf("KCT"), P.buf("VCA"), P.buf("AT"), P.buf("CT"))
    mix_hTt = [(AV(o, [128, 8, 512], BF), P.buf("mixhT%d" % i)) for i, o in enumerate((0, 8, 65, 73))]
    for s in range(nseq):
        for c in range(NCH):
            dma("sp", x_sb[:, c, :], x_d[s, c * 128:(c + 1) * 128, :], [], [xb[c]])
        if "A" in phases:
            ffn(0, 0)
        if dbg and s == 0:
            dma("sp", dbg_d["d_x1"].rearrange("(c p) d -> p c d", p=128), x_sb[:], list(xb), [])
        if "C" in phases:
            phase_c((qmb, keb, vtb, gtb, kvcb))
        if dbg and s == 0 and "C" in phases:
            dma("pool", dbg_d["d_kvc"], KVC, [kvcb], [])
            dma("sp", dbg_d["d_gates"], GT, [gtb], [])
            fence()
        if "D" in phases:
            phase_d((qmb, keb, vtb, gtb, kvcb, kctb, vcab))
        if dbg and s == 0 and "D" in phases:
            dma("pool", dbg_d["d_kct"], KCT[:, :, 0:127], [kctb], [])
            dma("pool", dbg_d["d_vca"], VCA[:, :, 0:97], [vcab], [])
            fence()
        if "E" in phases:
            phase_ef((qmb, keb, vtb, gtb, kvcb, kctb, vcab, atb))
        if dbg and s == 0 and "C" in phases:
            dma("pool", dbg_d["d_qm"], QM, [qmb], [])
            dma("pool", dbg_d["d_ke"], KE, [keb], [])
            dma("pool", dbg_d["d_v"], VT, [vtb], [])
            fence()
        if dbg and s == 0 and "E" in phases:
            dma("pool", dbg_d["d_aT"], AT, [atb], [])
            fence()
        if "H" in phases:
            phase_h((atb, ctb, mix_hTt))
        if dbg and s == 0 and "H" in phases:
            dma("pool", dbg_d["d_cT"], CT, [ctb], [])
            fence()
        if "I" in phases:
            phase_i((atb, ctb, mix_hTt))
        if dbg and s == 0:
            dma("sp", dbg_d["d_x2"].rearrange("(c p) d -> p c d", p=128), x_sb[:], list(xb), [])
        if "J" in phases:
            ffn(1, 2, fence_after=not (s + 1 < nseq and "A" in phases))
        for c in range(NCH):
            dma("sp", out_d[s, c * 128:(c + 1) * 128, :], x_sb[:, c, :], [xb[c]], [])
    P.emit()
    es.close()
    return nc, P


def make_in_maps(inp, nseq, n_cores):
    f = lambda a: np.ascontiguousarray(np.asarray(a, dtype=np.float32))
    x = f(inp["x"])
    shared = {
        "w_gate1": f(inp["ffn1_w_gate"][0]), "w_up1": f(inp["ffn1_w_up"][0]), "w_down1": f(inp["ffn1_w_down"][0]),
        "w_gate2": f(inp["ffn2_w_gate"][0]), "w_up2": f(inp["ffn2_w_up"][0]), "w_down2": f(inp["ffn2_w_down"][0]),
        "w_in": f(inp["w_in"][0]),
        "cmp_k_w1": f(inp["cmp_k_w1"][0]), "cmp_v_w1": f(inp["cmp_v_w1"][0]),
        "cmp_k_w2": f(inp["cmp_k_w2"][0]), "cmp_v_w2": f(inp["cmp_v_w2"][0]),
        "w_a": f(inp["w_attn_branch"][0]), "w_c": f(inp["w_conv_branch"][0]), "w_o": f(inp["w_out"][0]),
    }
    g3 = np.stack([f(inp["ffn1_norm_g"][0]), f(inp["mix_norm_g"][0]), f(inp["ffn2_norm_g"][0])], 0)
    shared["gT"] = np.ascontiguousarray(g3.reshape(3, 8, 128).transpose(2, 0, 1))
    qg = f(inp["q_norm_g"][0])
    kg = f(inp["k_norm_g"][0])
    gqk = np.concatenate([np.tile(qg, 8), np.tile(kg[1], 2), np.tile(kg[2], 2)])
    shared["gqk"] = np.ascontiguousarray(np.broadcast_to(gqk[None, :], (128, 768)))
    shared["gkc"] = np.ascontiguousarray(np.broadcast_to(np.tile(kg[0], 2)[None, :], (128, 128)))
    pek = f(inp["cmp_pe_k"][0])
    pev = f(inp["cmp_pe_v"][0])
    shared["peT"] = np.ascontiguousarray(np.stack([pek.T, pev.T], 1))
    cw = f(inp["conv_w"][0])
    shared["convw"] = np.ascontiguousarray(cw.reshape(3, 4, 128).transpose(2, 1, 0))
    shared.update(_const_tables())
    maps = []
    for i in range(n_cores):
        m = dict(shared)
        m["x"] = np.ascontiguousarray(x[i * nseq:(i + 1) * nseq])
        maps.append(m)
    return maps


_CACHE = {}


def kernel(**inputs):
    nseq = 4
    if "nc" not in _CACHE:
        _CACHE["nc"] = build(nseq)[0]
    nc = _CACHE["nc"]
    maps = make_in_maps(inputs, nseq, N_CORES)
    res = run_bass_kernel_spmd(nc, maps, core_ids=list(range(N_CORES)))
    out = np.concatenate([np.asarray(r["out"]) for r in res.results], axis=0)
    return out.astype(np.float32)
```
